# Optimizing a Trainium2 kernel written in Bass

```python
import math
import jax, jax.numpy as jnp
from jax import lax
import numpy as np

D_MODEL = 1024
BATCH = 4
SEQ = 4096
DEPTH = 4
DEC_BATCH = 32
DEC_SEQ = 64
PAST_LEN = 2048

CHUNK = 64
Q_BLOCK = 128
N_MEM = 256
D_FF = 4096
EPS = 1e-6

SSM_WIDTH = 256
SSM_GROUP = 16
SSM_GROUPS = SSM_WIDTH // SSM_GROUP
SSM_STATE = 64
SSM_DT_MIN = 1e-3
SSM_DT_MAX = 1e-1
ATT_HEADS = 4
ATT_HEAD_DIM = 64
ATT_HALF = ATT_HEAD_DIM // 2
ATT_WIDTH = ATT_HEADS * ATT_HEAD_DIM
CONV_WIDTH = 256
CONV_K = 31
GMLP_WIDTH = 256
GMLP_HEADS = 4
GMLP_HEAD_DIM = GMLP_WIDTH // GMLP_HEADS
GMLP_CHUNK = 128
X_HEADS = 4
X_HEAD_DIM = 128
X_WIDTH = X_HEADS * X_HEAD_DIM

OFF_SSM = 0
OFF_Q = OFF_SSM + SSM_WIDTH
OFF_K = OFF_Q + ATT_WIDTH
OFF_V = OFF_K + ATT_WIDTH
OFF_CONV = OFF_V + ATT_WIDTH
OFF_GMLP = OFF_CONV + 2 * CONV_WIDTH
IN_WIDTH = OFF_GMLP + 2 * GMLP_WIDTH
MIX_WIDTH = SSM_WIDTH + ATT_WIDTH + CONV_WIDTH + GMLP_WIDTH

kernel_name = 'hybrid_streaming_encoder_step'


def rms_norm(x, g):
    xf = x.astype(jnp.float32)
    y = xf * lax.rsqrt(jnp.mean(xf * xf, axis=-1, keepdims=True) + EPS)
    return (y * g.astype(jnp.float32)).astype(x.dtype)


def layer_norm(x, g, b):
    xf = x.astype(jnp.float32)
    mu = jnp.mean(xf, axis=-1, keepdims=True)
    var = jnp.mean(jnp.square(xf - mu), axis=-1, keepdims=True)
    y = (xf - mu) * lax.rsqrt(var + EPS) * g.astype(jnp.float32) + b.astype(jnp.float32)
    return y.astype(x.dtype)


def swiglu_ffn(h, w_gate, w_up, w_down):
    return (jax.nn.silu(h @ w_gate) * (h @ w_up)) @ w_down


def s5_mixer(u, s0_re, s0_im, a_re, a_im, b_re, b_im, c_re, c_im, d, log_dt, w_glu, b_glu):
    f32 = jnp.float32
    bt, t, _ = u.shape
    ug = u.astype(f32).reshape(bt, t, SSM_GROUPS, SSM_GROUP)
    lam = lax.complex(a_re.astype(f32), a_im.astype(f32))
    dt = jnp.exp(log_dt.astype(f32))[:, None]
    a_bar = jnp.exp(lam * dt)
    b_c = lax.complex(b_re.astype(f32), b_im.astype(f32))
    b_bar = ((a_bar - 1.0) / lam)[..., None] * b_c
    bu = jnp.einsum('btgc,gpc->btgp', ug.astype(jnp.complex64), b_bar)
    a_seq = jnp.broadcast_to(a_bar, bu.shape)

    def combine(e1, e2):
        return e1[0] * e2[0], e2[0] * e1[1] + e2[1]

    a_cum, s = lax.associative_scan(combine, (a_seq, bu), axis=1)
    s0 = lax.complex(s0_re.astype(f32), s0_im.astype(f32))
    s = s + a_cum * s0[:, None]
    c_c = lax.complex(c_re.astype(f32), c_im.astype(f32))
    y = jnp.real(jnp.einsum('btgp,gcp->btgc', s, c_c)) + d.astype(f32) * ug
    y = y.reshape(bt, t, SSM_WIDTH)
    g = jax.nn.gelu(y)
    out = g * jax.nn.sigmoid(g @ w_glu.astype(f32) + b_glu.astype(f32))
    s_last = s[:, -1]
    return out.astype(u.dtype), jnp.real(s_last).astype(s0_re.dtype), jnp.imag(s_last).astype(s0_re.dtype)


def alibi_slopes():
    return 2.0 ** (-8.0 * jnp.arange(1, ATT_HEADS + 1, dtype=jnp.float32) / ATT_HEADS)


def diff_attention(q, k, v, q_pos, k_pos, lam, lam_init, g_head):
    f32 = jnp.float32
    bt, tq = q.shape[0], q.shape[1]
    blk = min(tq, Q_BLOCK)
    nblk = tq // blk
    kf = k.astype(f32)
    vf = v.astype(f32)
    k1, k2 = kf[..., :ATT_HALF], kf[..., ATT_HALF:]
    slopes = alibi_slopes()
    k_chunk = k_pos // CHUNK
    scale = ATT_HALF ** -0.5

    def block(args):
        qb, qp = args
        qb = qb.astype(f32) * scale
        dist = jnp.abs(qp[:, None] - k_pos[None, :]).astype(f32)
        visible = k_chunk[None, :] <= (qp // CHUNK)[:, None]
        bias = jnp.where(visible[None], -slopes[:, None, None] * dist[None], -jnp.inf)
        a1 = jax.nn.softmax(jnp.einsum('bqhd,bkhd->bhqk', qb[..., :ATT_HALF], k1) + bias, axis=-1)
        a2 = jax.nn.softmax(jnp.einsum('bqhd,bkhd->bhqk', qb[..., ATT_HALF:], k2) + bias, axis=-1)
        return jnp.einsum('bhqk,bkhd->bqhd', a1 - lam * a2, vf)

    qb = q.reshape(bt, nblk, blk, ATT_HEADS, ATT_HEAD_DIM).transpose(1, 0, 2, 3, 4)
    qpb = q_pos.reshape(nblk, blk)
    o = lax.map(block, (qb, qpb))
    o = o.transpose(1, 0, 2, 3, 4).reshape(bt, tq, ATT_HEADS, ATT_HEAD_DIM)
    o = o * lax.rsqrt(jnp.mean(o * o, axis=-1, keepdims=True) + EPS) * g_head.astype(f32)
    o = o * (1.0 - lam_init)
    return o.reshape(bt, tq, ATT_WIDTH).astype(q.dtype)


def causal_depthwise_conv(z, buf, w, b):
    zp = jnp.concatenate([buf.astype(z.dtype), z], axis=1)
    y = lax.conv_general_dilated(zp, w[:, None, :].astype(z.dtype), window_strides=(1,), padding='VALID',
                                 dimension_numbers=('NWC', 'WIO', 'NWC'), feature_group_count=z.shape[-1])
    return y + b.astype(z.dtype), zp[:, -(CONV_K - 1):]


def conformer_conv(p, buf, w, b, ln_g, ln_b, w_pw):
    z = p[..., :CONV_WIDTH] * jax.nn.sigmoid(p[..., CONV_WIDTH:])
    y, new_buf = causal_depthwise_conv(z, buf, w, b)
    y = jax.nn.silu(layer_norm(y, ln_g, ln_b))
    return y @ w_pw, new_buf


def chunk_spatial_gating(p, ln_g, ln_b, ws, bs):
    bt, t, _ = p.shape
    z = jax.nn.gelu(p)
    u, v = z[..., :GMLP_WIDTH], z[..., GMLP_WIDTH:]
    v = layer_norm(v, ln_g, ln_b)
    L = min(t, GMLP_CHUNK)
    nc = t // L
    mask = jnp.tril(jnp.ones((L, L), dtype=bool))
    w = jnp.where(mask, ws[:, :L, :L], 0.0).astype(v.dtype)
    vc = v.reshape(bt, nc, L, GMLP_HEADS, GMLP_HEAD_DIM)
    mixed = jnp.einsum('hij,bcjhd->bcihd', w, vc) + bs[:, :L].T.astype(v.dtype)[None, None, :, :, None]
    return u * mixed.reshape(bt, t, GMLP_WIDTH), v


def memory_cross_attention(h, mem_k, mem_v, wq, wo):
    f32 = jnp.float32
    bt, t, _ = h.shape
    q = (h @ wq).reshape(bt, t, X_HEADS, X_HEAD_DIM).astype(f32) * (X_HEAD_DIM ** -0.5)
    a = jax.nn.softmax(jnp.einsum('bqhd,bkhd->bhqk', q, mem_k.astype(f32)), axis=-1)
    o = jnp.einsum('bhqk,bkhd->bqhd', a, mem_v.astype(f32)).reshape(bt, t, X_WIDTH)
    return o.astype(h.dtype) @ wo


def encoder_layer(x, prm, lam_init, attn_past, s0_re, s0_im, conv_buf, mem_k, mem_v):
    f32 = jnp.float32
    bt, t, _ = x.shape
    x = x + 0.5 * swiglu_ffn(rms_norm(x, prm['ffn1_norm']), prm['ffn1_w_gate'], prm['ffn1_w_up'], prm['ffn1_w_down'])
    h = rms_norm(x, prm['mix_norm'])
    proj = h @ prm['w_in']
    a_out, s_re, s_im = s5_mixer(proj[..., OFF_SSM:OFF_Q], s0_re, s0_im, prm['ssm_a_re'], prm['ssm_a_im'],
                                 prm['ssm_b_re'], prm['ssm_b_im'], prm['ssm_c_re'], prm['ssm_c_im'],
                                 prm['ssm_d'], prm['ssm_log_dt'], prm['ssm_w_glu'], prm['ssm_b_glu'])
    q = proj[..., OFF_Q:OFF_K].reshape(bt, t, ATT_HEADS, ATT_HEAD_DIM)
    k = proj[..., OFF_K:OFF_V].reshape(bt, t, ATT_HEADS, ATT_HEAD_DIM)
    v = proj[..., OFF_V:OFF_CONV].reshape(bt, t, ATT_HEADS, ATT_HEAD_DIM)
    if attn_past is None:
        past = 0
        k_all, v_all = k, v
    else:
        past = attn_past[0].shape[1]
        k_all = jnp.concatenate([attn_past[0].astype(k.dtype), k], axis=1)
        v_all = jnp.concatenate([attn_past[1].astype(v.dtype), v], axis=1)
    q_pos = past + jnp.arange(t, dtype=jnp.int32)
    k_pos = jnp.arange(past + t, dtype=jnp.int32)
    lam = (jnp.exp(jnp.dot(prm['lq1'].astype(f32), prm['lk1'].astype(f32)))
           - jnp.exp(jnp.dot(prm['lq2'].astype(f32), prm['lk2'].astype(f32))) + lam_init)
    b_out = diff_attention(q, k_all, v_all, q_pos, k_pos, lam, lam_init, prm['dattn_norm'])
    c_out, new_buf = conformer_conv(proj[..., OFF_CONV:OFF_GMLP], conv_buf, prm['conv_w'], prm['conv_b'],
                                    prm['conv_ln_g'], prm['conv_ln_b'], prm['conv_w_pw'])
    d_out, gmlp_v = chunk_spatial_gating(proj[..., OFF_GMLP:IN_WIDTH], prm['gmlp_ln_g'], prm['gmlp_ln_b'],
                                         prm['gmlp_ws'], prm['gmlp_bs'])
    mix = jnp.concatenate([a_out, b_out, c_out.astype(x.dtype), d_out.astype(x.dtype)], axis=-1)
    x = x + mix @ prm['w_out']
    x = x + memory_cross_attention(rms_norm(x, prm['xattn_norm']), mem_k, mem_v, prm['xattn_wq'], prm['xattn_wo'])
    x = x + 0.5 * swiglu_ffn(rms_norm(x, prm['ffn2_norm']), prm['ffn2_w_gate'], prm['ffn2_w_up'], prm['ffn2_w_down'])
    return x, k, v, s_re, s_im, new_buf, gmlp_v


def setup_inputs(seed: int = 0) -> dict:
    key = jax.random.key(seed)
    ks = iter(jax.random.split(key, 80))
    f32 = jnp.float32
    L = DEPTH

    def nrm(shape, scale=1.0):
        return scale * jax.random.normal(next(ks), shape, f32)

    def gain(shape):
        return 1.0 + 0.05 * jax.random.normal(next(ks), shape, f32)

    n_idx = jnp.arange(SSM_STATE, dtype=f32)
    return {
        'x_prompt': nrm((BATCH, SEQ, D_MODEL)),
        'x_sample': nrm((DEC_BATCH, DEC_SEQ, D_MODEL)),
        'mem_prompt': nrm((BATCH, N_MEM, D_MODEL)),
        'cache_attn_k': nrm((L, DEC_BATCH, PAST_LEN, ATT_HEADS, ATT_HEAD_DIM)),
        'cache_attn_v': nrm((L, DEC_BATCH, PAST_LEN, ATT_HEADS, ATT_HEAD_DIM)),
        'state_ssm_re': nrm((L, DEC_BATCH, SSM_GROUPS, SSM_STATE), 0.5),
        'state_ssm_im': nrm((L, DEC_BATCH, SSM_GROUPS, SSM_STATE), 0.5),
        'state_conv': nrm((L, DEC_BATCH, CONV_K - 1, CONV_WIDTH), 0.5),
        'cache_mem_k': nrm((L, DEC_BATCH, N_MEM, X_HEADS, X_HEAD_DIM)),
        'cache_mem_v': nrm((L, DEC_BATCH, N_MEM, X_HEADS, X_HEAD_DIM)),
        'ffn1_norm': gain((L, D_MODEL)),
        'ffn1_w_gate': nrm((L, D_MODEL, D_FF), D_MODEL ** -0.5),
        'ffn1_w_up': nrm((L, D_MODEL, D_FF), D_MODEL ** -0.5),
        'ffn1_w_down': nrm((L, D_FF, D_MODEL), D_FF ** -0.5),
        'mix_norm': gain((L, D_MODEL)),
        'w_in': nrm((L, D_MODEL, IN_WIDTH), D_MODEL ** -0.5),
        'w_out': nrm((L, MIX_WIDTH, D_MODEL), MIX_WIDTH ** -0.5),
        'ssm_a_re': -0.5 + 0.01 * nrm((L, SSM_GROUPS, SSM_STATE)),
        'ssm_a_im': math.pi * n_idx + 0.01 * nrm((L, SSM_GROUPS, SSM_STATE)),
        'ssm_b_re': nrm((L, SSM_GROUPS, SSM_STATE, SSM_GROUP), (2 * SSM_GROUP) ** -0.5),
        'ssm_b_im': nrm((L, SSM_GROUPS, SSM_STATE, SSM_GROUP), (2 * SSM_GROUP) ** -0.5),
        'ssm_c_re': nrm((L, SSM_GROUPS, SSM_GROUP, SSM_STATE), (2 * SSM_STATE) ** -0.5),
        'ssm_c_im': nrm((L, SSM_GROUPS, SSM_GROUP, SSM_STATE), (2 * SSM_STATE) ** -0.5),
        'ssm_d': nrm((L, SSM_GROUPS, SSM_GROUP)),
        'ssm_log_dt': jax.random.uniform(next(ks), (L, SSM_GROUPS), f32, math.log(SSM_DT_MIN), math.log(SSM_DT_MAX)),
        'ssm_w_glu': nrm((L, SSM_WIDTH, SSM_WIDTH), SSM_WIDTH ** -0.5),
        'ssm_b_glu': nrm((L, SSM_WIDTH), 0.02),
        'dattn_lq1': nrm((L, ATT_HALF), 0.1),
        'dattn_lk1': nrm((L, ATT_HALF), 0.1),
        'dattn_lq2': nrm((L, ATT_HALF), 0.1),
        'dattn_lk2': nrm((L, ATT_HALF), 0.1),
        'dattn_norm': gain((L, ATT_HEAD_DIM)),
        'conv_w': nrm((L, CONV_K, CONV_WIDTH), CONV_K ** -0.5),
        'conv_b': nrm((L, CONV_WIDTH), 0.02),
        'conv_ln_g': gain((L, CONV_WIDTH)),
        'conv_ln_b': nrm((L, CONV_WIDTH), 0.02),
        'conv_w_pw': nrm((L, CONV_WIDTH, CONV_WIDTH), CONV_WIDTH ** -0.5),
        'gmlp_ln_g': gain((L, GMLP_WIDTH)),
        'gmlp_ln_b': nrm((L, GMLP_WIDTH), 0.02),
        'gmlp_ws': nrm((L, GMLP_HEADS, GMLP_CHUNK, GMLP_CHUNK), GMLP_CHUNK ** -0.5),
        'gmlp_bs': gain((L, GMLP_HEADS, GMLP_CHUNK)),
        'xattn_norm': gain((L, D_MODEL)),
        'mem_norm': gain((L, D_MODEL)),
        'xattn_wq': nrm((L, D_MODEL, X_WIDTH), D_MODEL ** -0.5),
        'xattn_wk': nrm((L, D_MODEL, X_WIDTH), D_MODEL ** -0.5),
        'xattn_wv': nrm((L, D_MODEL, X_WIDTH), D_MODEL ** -0.5),
        'xattn_wo': nrm((L, X_WIDTH, D_MODEL), X_WIDTH ** -0.5),
        'ffn2_norm': gain((L, D_MODEL)),
        'ffn2_w_gate': nrm((L, D_MODEL, D_FF), D_MODEL ** -0.5),
        'ffn2_w_up': nrm((L, D_MODEL, D_FF), D_MODEL ** -0.5),
        'ffn2_w_down': nrm((L, D_FF, D_MODEL), D_FF ** -0.5),
        'final_norm': gain((D_MODEL,)),
    }


def reference(x_prompt, x_sample, mem_prompt, cache_attn_k, cache_attn_v, state_ssm_re, state_ssm_im,
              state_conv, cache_mem_k, cache_mem_v,
              ffn1_norm, ffn1_w_gate, ffn1_w_up, ffn1_w_down, mix_norm, w_in, w_out,
              ssm_a_re, ssm_a_im, ssm_b_re, ssm_b_im, ssm_c_re, ssm_c_im, ssm_d, ssm_log_dt, ssm_w_glu, ssm_b_glu,
              dattn_lq1, dattn_lk1, dattn_lq2, dattn_lk2, dattn_norm,
              conv_w, conv_b, conv_ln_g, conv_ln_b, conv_w_pw,
              gmlp_ln_g, gmlp_ln_b, gmlp_ws, gmlp_bs,
              xattn_norm, mem_norm, xattn_wq, xattn_wk, xattn_wv, xattn_wo,
              ffn2_norm, ffn2_w_gate, ffn2_w_up, ffn2_w_down, final_norm):
    xp, xs = x_prompt, x_sample
    bp = x_prompt.shape[0]
    p_k, p_v, p_sre, p_sim, p_conv, p_mk, p_mv = [], [], [], [], [], [], []
    s_k, s_v, s_sre, s_sim, s_conv, s_gv = [], [], [], [], [], []
    for l in range(DEPTH):
        prm = dict(ffn1_norm=ffn1_norm[l], ffn1_w_gate=ffn1_w_gate[l], ffn1_w_up=ffn1_w_up[l], ffn1_w_down=ffn1_w_down[l],
                   mix_norm=mix_norm[l], w_in=w_in[l], w_out=w_out[l],
                   ssm_a_re=ssm_a_re[l], ssm_a_im=ssm_a_im[l], ssm_b_re=ssm_b_re[l], ssm_b_im=ssm_b_im[l],
                   ssm_c_re=ssm_c_re[l], ssm_c_im=ssm_c_im[l], ssm_d=ssm_d[l], ssm_log_dt=ssm_log_dt[l],
                   ssm_w_glu=ssm_w_glu[l], ssm_b_glu=ssm_b_glu[l],
                   lq1=dattn_lq1[l], lk1=dattn_lk1[l], lq2=dattn_lq2[l], lk2=dattn_lk2[l], dattn_norm=dattn_norm[l],
                   conv_w=conv_w[l], conv_b=conv_b[l], conv_ln_g=conv_ln_g[l], conv_ln_b=conv_ln_b[l], conv_w_pw=conv_w_pw[l],
                   gmlp_ln_g=gmlp_ln_g[l], gmlp_ln_b=gmlp_ln_b[l], gmlp_ws=gmlp_ws[l], gmlp_bs=gmlp_bs[l],
                   xattn_norm=xattn_norm[l], xattn_wq=xattn_wq[l], xattn_wo=xattn_wo[l],
                   ffn2_norm=ffn2_norm[l], ffn2_w_gate=ffn2_w_gate[l], ffn2_w_up=ffn2_w_up[l], ffn2_w_down=ffn2_w_down[l])
        lam_init = 0.8 - 0.6 * math.exp(-0.3 * l)
        mem_h = rms_norm(mem_prompt, mem_norm[l])
        mk = (mem_h @ xattn_wk[l]).reshape(bp, N_MEM, X_HEADS, X_HEAD_DIM)
        mv = (mem_h @ xattn_wv[l]).reshape(bp, N_MEM, X_HEADS, X_HEAD_DIM)
        zs = jnp.zeros((bp, SSM_GROUPS, SSM_STATE), xp.dtype)
        zb = jnp.zeros((bp, CONV_K - 1, CONV_WIDTH), xp.dtype)
        xp, kp_, vp_, srp, sip, cbp, _ = encoder_layer(xp, prm, lam_init, None, zs, zs, zb, mk, mv)
        p_k.append(kp_); p_v.append(vp_); p_sre.append(srp); p_sim.append(sip)
        p_conv.append(cbp); p_mk.append(mk); p_mv.append(mv)
        xs, ks_, vs_, srs, sis, cbs, gvs = encoder_layer(
            xs, prm, lam_init, (cache_attn_k[l], cache_attn_v[l]), state_ssm_re[l], state_ssm_im[l],
            state_conv[l], cache_mem_k[l], cache_mem_v[l])
        s_k.append(ks_); s_v.append(vs_); s_sre.append(srs); s_sim.append(sis)
        s_conv.append(cbs); s_gv.append(gvs)
    y_prompt = rms_norm(xp, final_norm)
    y_sample = rms_norm(xs, final_norm)
    return (y_prompt, y_sample,
            jnp.stack(p_k), jnp.stack(p_v), jnp.stack(p_sre), jnp.stack(p_sim), jnp.stack(p_conv),
            jnp.stack(p_mk), jnp.stack(p_mv),
            jnp.stack(s_k), jnp.stack(s_v), jnp.stack(s_sre), jnp.stack(s_sim), jnp.stack(s_conv),
            jnp.stack(s_gv))
```

```python
import math
import contextlib
import numpy as np
import concourse.bass as bass
import concourse.mybir as mybir
from concourse.bass_utils import run_bass_kernel_spmd

F32 = mybir.dt.float32
BF16 = mybir.dt.bfloat16
ALU = mybir.AluOpType
AF = mybir.ActivationFunctionType

COMPUTE = ("pe", "act", "dve", "pool")
NDMA_SEMS = 24
EPS = 1e-6
L_FULL = 4
TP_FULL = 4096
PAST = 2048
NSS = 4
TS = 64
D = 1024
DFF = 4096


class Buf:
    __slots__ = ("name", "w", "r", "psum")

    def __init__(self, name="", psum=False):
        self.name = name
        self.w = None
        self.r = []
        self.psum = psum


class T:
    __slots__ = ("ap", "buf")

    def __init__(self, ap, buf):
        self.ap = ap
        self.buf = buf

    def __getitem__(self, k):
        return T(self.ap[k], self.buf)

    def re(self, pat, **kw):
        return T(self.ap.rearrange(pat, **kw), self.buf)

    def sub(self, buf):
        return T(self.ap, buf)


class Sched:
    def __init__(self, nc):
        self.nc = nc
        self.streams = {e: [] for e in ("pe", "act", "dve", "pool", "sp")}
        self.count = {e: 0 for e in COMPUTE}
        self.waited = {e: {} for e in self.streams}
        self.dma_cnt = {}
        self.dma_rr = {"sp": 0, "pool": 0, "act": 0}

    def _need(self, eng, tok, waits):
        if tok is None:
            return
        key, val, teng = tok
        if teng == eng and eng == "pe":
            return
        if self.waited[eng].get(key, 0) >= val:
            return
        if waits.get(key, 0) < val:
            waits[key] = val

    def _deps(self, eng, reads, writes):
        waits = {}
        for b in reads:
            self._need(eng, b.w, waits)
            if b.psum:
                for t in b.r:
                    if t[2] != eng:
                        self._need(eng, t, waits)
        for b in writes:
            self._need(eng, b.w, waits)
            for t in b.r:
                self._need(eng, t, waits)
        for k, v in waits.items():
            self.waited[eng][k] = v
        return list(waits.items())

    def _mark(self, tok, reads, writes):
        for b in reads:
            b.r.append(tok)
            if len(b.r) > 16:
                best = {}
                for t in b.r:
                    if t[0] not in best or best[t[0]][1] < t[1]:
                        best[t[0]] = t
                b.r = list(best.values())
        for b in writes:
            b.w = tok
            b.r = []

    def op(self, eng, fn, reads=(), writes=()):
        waits = self._deps(eng, reads, writes)
        self.count[eng] += 1
        tok = (eng, self.count[eng], eng)
        self.streams[eng].append((waits, fn, (eng, 1)))
        self._mark(tok, reads, writes)

    def dma(self, q, fn, reads=(), writes=(), grp=None, ngrp=6):
        if grp is None:
            j = self.dma_rr[q]
            self.dma_rr[q] = (j + 1) % NDMA_SEMS
            key = "dma_%s_%d" % (q, j)
        else:
            j = self.dma_rr.get(grp, 0)
            self.dma_rr[grp] = (j + 1) % ngrp
            key = "dma_%s_%d" % (grp, j)
        prev = self.dma_cnt.get(key, 0)
        waits = dict(self._deps(q, reads, writes))
        if prev > 0 and self.waited[q].get(key, 0) < 16 * prev:
            waits[key] = max(waits.get(key, 0), 16 * prev)
            self.waited[q][key] = 16 * prev
        self.dma_cnt[key] = prev + 1
        tok = (key, 16 * (prev + 1), "dma")
        self.streams[q].append((list(waits.items()), fn, (key, 16)))
        self._mark(tok, reads, writes)
        return tok

    def barrier(self, with_dma=True):
        for e in COMPUTE + ("sp",):
            waits = {}
            for o in COMPUTE:
                if o != e and self.count[o] > self.waited[e].get(o, 0):
                    waits[o] = self.count[o]
            if with_dma:
                for key, cnt in self.dma_cnt.items():
                    if self.waited[e].get(key, 0) < 16 * cnt:
                        waits[key] = 16 * cnt
            for k, v in waits.items():
                self.waited[e][k] = v
            if waits:
                self.streams[e].append((list(waits.items()), None, None))

    def final_wait_all(self, eng="sp"):
        waits = []
        for key, cnt in self.dma_cnt.items():
            waits.append((key, 16 * cnt))
        for e in COMPUTE:
            if self.count[e] > 0:
                waits.append((e, self.count[e]))
        self.streams[eng].append((waits, None, None))

    def emit(self):
        nc = self.nc
        keys = set(COMPUTE)
        for k in self.dma_cnt:
            keys.add(k)
        keys = sorted(keys)
        with contextlib.ExitStack() as st:
            sems = {k: st.enter_context(nc.semaphore("s_" + k)) for k in keys}
            block = st.enter_context(nc.Block())
            engs = {"pe": block.tensor, "act": block.scalar, "dve": block.vector,
                    "pool": block.gpsimd, "sp": block.sync}

            def mk(name):
                items = self.streams[name]

                def body(e):
                    for waits, fn, inc in items:
                        for k, v in waits:
                            e.wait_ge(sems[k], v)
                        if fn is not None:
                            fn(e).then_inc(sems[inc[0]], inc[1])
                return body
            for name in ("pe", "act", "dve", "pool", "sp"):
                engs[name](mk(name))


class Prog:
    def __init__(self, TP, L):
        self.TP, self.L = TP, L
        self.nc = bass.Bass("TRN2", target_bir_lowering=False)
        self.S = Sched(self.nc)
        self.st = contextlib.ExitStack()
        self.dr = {}

    def din(self, name, shape):
        t = self.nc.dram_tensor(name, list(shape), F32, kind="ExternalInput").ap()
        self.dr[name] = T(t, Buf(name))
        return self.dr[name]

    def dout(self, name, shape):
        t = self.nc.dram_tensor(name, list(shape), F32, kind="ExternalOutput").ap()
        self.dr[name] = T(t, Buf(name))
        return self.dr[name]

    def dscratch(self, name, shape, dt=F32):
        t = self.nc.dram_tensor(name, list(shape), dt, kind="Internal").ap()
        self.dr[name] = T(t, Buf(name))
        return self.dr[name]

    def sb(self, name, shape, dt=F32):
        t = self.st.enter_context(self.nc.sbuf_tensor(name, list(shape), dt))
        return T(t[:], Buf(name))

    def MM(self, o, l, r, start=True, stop=True, **kw):
        self.S.op("pe", lambda e: e.matmul(o.ap, lhsT=l.ap, rhs=r.ap, start=start, stop=stop, **kw),
                  reads=[l.buf, r.buf], writes=[o.buf])

    def ACT(self, o, i, func, bias=None, scale=None, eng="act"):
        reads = [i.buf]
        kw = {}
        if bias is not None:
            if isinstance(bias, T):
                reads.append(bias.buf)
                kw["bias"] = bias.ap
            else:
                kw["bias"] = float(bias)
        if scale is not None:
            if isinstance(scale, T):
                reads.append(scale.buf)
                kw["scale"] = scale.ap
            else:
                kw["scale"] = float(scale)
        self.S.op("act", lambda e: e.activation(out=o.ap, in_=i.ap, func=func, **kw), reads=reads, writes=[o.buf])

    def TT(self, o, a, b, op, eng="dve"):
        self.S.op(eng, lambda e: e.tensor_tensor(out=o.ap, in0=a.ap, in1=b.ap, op=op), reads=[a.buf, b.buf], writes=[o.buf])

    def TS(self, o, a, s1, s2=None, op0=ALU.mult, op1=None, eng="dve"):
        reads = [a.buf]
        v1 = s1
        v2 = s2
        if isinstance(s1, T):
            reads.append(s1.buf)
            v1 = s1.ap
        if isinstance(s2, T):
            reads.append(s2.buf)
            v2 = s2.ap
        if op1 is None:
            self.S.op(eng, lambda e: e.tensor_scalar(out=o.ap, in0=a.ap, scalar1=v1, scalar2=None, op0=op0), reads=reads, writes=[o.buf])
        else:
            self.S.op(eng, lambda e: e.tensor_scalar(out=o.ap, in0=a.ap, scalar1=v1, scalar2=v2, op0=op0, op1=op1), reads=reads, writes=[o.buf])

    def STT(self, o, a, s, b, op0, op1, eng="dve"):
        reads = [a.buf, b.buf]
        v = s
        if isinstance(s, T):
            reads.append(s.buf)
            v = s.ap
        self.S.op(eng, lambda e: e.scalar_tensor_tensor(out=o.ap, in0=a.ap, scalar=v, in1=b.ap, op0=op0, op1=op1), reads=reads, writes=[o.buf])

    def CP(self, o, i, eng="dve", real=False):
        if real or eng != "dve":
            self.S.op(eng, lambda e: e.tensor_copy(out=o.ap, in_=i.ap), reads=[i.buf], writes=[o.buf])
        else:
            self.S.op(eng, lambda e: e.tensor_scalar(out=o.ap, in0=i.ap, scalar1=1.0, scalar2=None, op0=ALU.mult), reads=[i.buf], writes=[o.buf])

    def RECIP(self, o, i):
        self.S.op("dve", lambda e: e.reciprocal(out=o.ap, in_=i.ap), reads=[i.buf], writes=[o.buf])

    def MEMSET(self, o, v, eng="dve"):
        self.S.op(eng, lambda e: e.memset(o.ap, v), writes=[o.buf])

    def SCAN(self, o, d0, d1, init):
        reads = [d0.buf, d1.buf]
        iv = init
        if isinstance(init, T):
            reads.append(init.buf)
            iv = init.ap
        self.S.op("dve", lambda e: e.tensor_tensor_scan(out=o.ap, data0=d0.ap, data1=d1.ap, initial=iv, op0=ALU.mult, op1=ALU.add),
                  reads=reads, writes=[o.buf])

    def DMA(self, q, o, i, grp=None):
        if len(o.ap.shape) > 3 and len(i.ap.shape) == len(o.ap.shape):
            names = " ".join("d%d" % k for k in range(1, len(o.ap.shape)))
            pat = "p %s -> p (%s)" % (names, names)
            try:
                o2, i2 = o.re(pat), i.re(pat)
                o, i = o2, i2
            except Exception:
                pass
        self.S.dma(q, lambda e: e.dma_start(out=o.ap, in_=i.ap), reads=[i.buf], writes=[o.buf], grp=grp)


def build(TP=TP_FULL, L=L_FULL):
    P = Prog(TP, L)
    phc = [0]

    def phase_done():
        phc[0] += 1
        if _LIMIT is not None and phc[0] >= _LIMIT:
            raise StopBuild()
    nc, S = P.nc, P.S
    NT = TP // 512
    NKT = TP // 128
    NS = NSS * TS

    xT_p = P.din("xT_p", [D, TP])
    xT_s = P.din("xT_s", [D, NS])
    memT = P.din("memT", [D, 256])
    ckT = P.din("ckT", [L, NSS, 256, PAST])
    cv = P.din("cv", [L, NSS, PAST, 256])
    s0 = P.din("s0", [128, L, NSS, 16])
    cbuf = P.din("cbuf", [128, L, 2, NSS, 30])
    cmkT = P.din("cmkT", [L, NSS, 512, 256])
    cmv = P.din("cmv", [L, NSS, 256, 512])
    Wn = {}
    for nm, shp in [("ffn1_w_gate", [L, D, DFF]), ("ffn1_w_up", [L, D, DFF]), ("ffn1_w_down", [L, DFF, D]),
                    ("ffn2_w_gate", [L, D, DFF]), ("ffn2_w_up", [L, D, DFF]), ("ffn2_w_down", [L, DFF, D]),
                    ("w_in", [L, D, 2048]), ("w_out", [L, D, D]), ("xattn_wq", [L, D, 512]), ("xattn_wk", [L, D, 512]),
                    ("xattn_wv", [L, D, 512]), ("xattn_wo", [L, 512, D]), ("ssm_w_glu", [L, 256, 256]),
                    ("conv_w_pw", [L, 256, 256])]:
        Wn[nm] = P.din(nm, shp)
    gains = P.din("gains", [128, L, 5, 8])
    fin_g = P.din("fin_g", [128, 8])
    Bx_d = P.din("Bx", [L, 128, 16, 128])
    Bxs_d = P.din("Bxs", [L, 128, 16, 128])
    Cx1_d = P.din("Cx1", [L, 128, 16, 128])
    Cx2_d = P.din("Cx2", [L, 128, 16, 128])
    ssm_sc = P.din("ssm_sc", [128, L, 3, 16])
    colp = P.din("colp", [128, L, 7, 2])
    convw = P.din("convw", [128, L, 2, 31])
    gm_gb = P.din("gm_gb", [128, L, 2, 256])
    wsT = P.din("wsT", [L, 128, 4, 128])
    bs_bc = P.din("bs_bc", [128, L, 2, 128])
    bs_bc_s = P.din("bs_bc_s", [128, L, 2, 128])
    lqk = P.din("lqk", [1, L, 4, 32])
    dn = P.din("dn", [64, L])
    c_ident = P.din("c_ident", [128, 128])
    c_alibi = P.din("c_alibi", [128, 4, 33])
    c_diag = P.din("c_diag", [128, 4, 128])
    c_tril = P.din("c_tril", [128, 128])
    c_sel = P.din("c_sel", [65, 64])
    c_misc = P.din("c_misc", [128, 66])
    c_lam = P.din("c_lam", [64, L, 2])

    yT_p = P.dout("yT_p", [D, TP])
    yT_s = P.dout("yT_s", [D, NS])
    o_kT_p = P.dout("o_kT_p", [L, 256, TP])
    o_v_p = P.dout("o_v_p", [L, TP, 256])
    o_ssm_p = P.dout("o_ssm_p", [L, 128, 16])
    o_conv_p = P.dout("o_conv_p", [L, 256, 30])
    o_mkT = P.dout("o_mkT", [L, 512, 256])
    o_mv = P.dout("o_mv", [L, 256, 512])
    o_kT_s = P.dout("o_kT_s", [L, 256, NS])
    o_v_s = P.dout("o_v_s", [L, NS, 256])
    o_ssm_s = P.dout("o_ssm_s", [L, NSS, 128, 16])
    o_conv_s = P.dout("o_conv_s", [L, NSS, 256, 30])
    o_gv_s = P.dout("o_gv_s", [L, NS, 256])
    s5tab = P.dscratch("s5tab", [L, 128, 5 * 1024])
    kth_d = P.dscratch("kth_d", [L, 128, 2, TP], BF16)
    vah_d = P.dscratch("vah_d", [L, 128, NKT, 260], BF16)

    x = P.sb("x", [128, 8, 512])
    h = P.sb("h", [128, 8, 512], BF16)
    NRING = 4
    ring = [P.sb("ring%d" % i, [128, 4096], BF16) for i in range(NRING)]
    ident_f = P.sb("ident_f", [128, 128])
    ident_b = P.sb("ident_b", [128, 128], BF16)
    ones_b = P.sb("ones_b", [128, 128], BF16)
    onesf64 = P.sb("onesf64", [64, 64])
    onesf128 = P.sb("onesf128", [128, 128])
    alibi = P.sb("alibi", [128, 4, 33])
    diag = P.sb("diag", [128, 4, 128])
    tril_b = P.sb("tril_b", [128, 128], BF16)
    sel = P.sb("sel", [65, 64])
    misc = P.sb("misc", [128, 66])
    gains_sb = P.sb("gains_sb", [128, L, 5, 8])
    fin_sb = P.sb("fin_sb", [128, 8])
    colp_sb = P.sb("colp_sb", [128, L, 7, 2])
    convw_sb = P.sb("convw_sb", [128, L, 2, 31])
    bs_sb = P.sb("bs_sb", [128, L, 2, 128])
    bs_sb_s = P.sb("bs_sb_s", [128, L, 2, 128])
    dn_sb = P.sb("dn_sb", [64, L])
    clam_sb = P.sb("clam_sb", [64, L, 2])
    lamneg = P.sb("lamneg", [64, L])
    gcol = P.sb("gcol", [64, L])
    Etab = P.sb("Etab", [128, L, 2, 16])
    sst = [P.sb("sst%d" % l, [128, 16]) for l in range(L)]
    ctail = [P.sb("ctail%d" % l, [128, 2, 30], BF16) for l in range(L)]
    KTh = P.sb("KTh", [128, 2, TP], BF16)
    VAh = P.sb("VAh", [128, NKT, 4, 65], BF16)
    KTown_p = P.sb("KTown", [128, 2, NS], BF16)
    VAown_p = P.sb("VAown", [64, NSS, 4, 65], BF16)
    u_b = P.sb("u_b", [128, 2, 512], BF16)
    u_f = P.sb("u_f", [128, 2, 512])
    QT = P.sb("QT", [128, 2, 512], BF16)
    zp = P.sb("zp", [128, 2, NSS * 94 + 200], BF16)
    gu = P.sb("gu", [128, 2, 512])
    mix = P.sb("mix", [128, 8, 512], BF16)
    ARENA_F32 = 16 * 1024
    arena = P.sb("arena", [128, ARENA_F32])
    PS = P.st.enter_context(nc.psum_tensor("PS", [128, 8, 512], F32))
    psb = [Buf("ps%d" % i, psum=True) for i in range(8)]

    def ps(b, parts=128, cols=512, c0=0):
        return T(PS[0:parts, b, c0:c0 + cols], psb[b])

    class Arena:
        def __init__(self):
            self.off = 0

        def reset(self):
            S.barrier()
            self.off = 0

        def alloc(self, n_elems, dt=F32, parts=128, name=""):
            nf = n_elems if dt == F32 else (n_elems + 1) // 2
            assert self.off + nf <= ARENA_F32, (name, self.off, nf)
            ap = arena.ap[0:parts, self.off:self.off + nf]
            self.off += nf
            if dt != F32:
                ap = ap.bitcast(dt)
            return T(ap, Buf(name))
    A = Arena()

    WB = {}
    for nm_ in ["ffn1_w_gate", "ffn1_w_up", "ffn1_w_down", "ffn2_w_gate", "ffn2_w_up", "ffn2_w_down", "w_in", "w_out",
                "xattn_wq", "xattn_wk", "xattn_wv", "xattn_wo"]:
        shp_ = list(Wn[nm_].ap.shape)
        tb_ = nc.dram_tensor(nm_ + "_b16", shp_, BF16, kind="Internal").ap()
        WB[nm_] = [T(tb_[l_], Buf("%s_b16_%d" % (nm_, l_))) for l_ in range(L)]

    def convert_weights(l_, only=None):
        order = ["ffn1_w_gate", "ffn1_w_up", "ffn1_w_down", "w_in", "w_out", "xattn_wq", "xattn_wo",
                 "ffn2_w_gate", "ffn2_w_up", "ffn2_w_down"]
        if only is not None:
            order = only
        if True:
            for nm_ in order:
                src = T(Wn[nm_].ap[l_], Wn[nm_].buf)
                dst = WB[nm_][l_]
                rows, cols = src.ap.shape
                if cols > 2048:
                    src = src.re("r (a b) -> r a b", b=2048)
                    dst = dst.re("r (a b) -> r a b", b=2048)
                half = rows // 2
                P.DMA("pool", dst[0:half], src[0:half], grp="cv")
                P.DMA("pool", dst[half:rows], src[half:rows], grp="cv")

    wq_list = []
    wstate = {"issued": 0, "taken": 0}

    def w_issue_upto(n):
        while wstate["issued"] < min(n, len(wq_list)):
            i = wstate["issued"]
            tag, src, shp = wq_list[i]
            slot = ring[i % NRING]
            ne = 1
            for s_ in shp:
                ne *= s_
            dst = slot[:, 0:ne]
            if len(shp) == 2:
                dst = dst.re("p (a b) -> p a b", a=shp[0])
            P.DMA("sp", dst, src)
            wstate["issued"] += 1

    def w_next(tag):
        i = wstate["taken"]
        assert wq_list[i][0] == tag, (wq_list[i][0], tag)
        w_issue_upto(i + NRING - 1)
        wstate["taken"] += 1
        shp = wq_list[i][2]
        ne = 1
        for s_ in shp:
            ne *= s_
        v = ring[i % NRING][:, 0:ne]
        if len(shp) == 2:
            v = v.re("p (a b) -> p a b", a=shp[0])
        return v

    def kview(w, l, r0, nr, c0, ncol):
        return T(w.ap[l, r0:r0 + nr, c0:c0 + ncol].rearrange("(k p) n -> p k n", p=128), w.buf)

    def bview(nm, l, r0, nr, c0, ncol):
        wb = WB[nm][l]
        return T(wb.ap[r0:r0 + nr, c0:c0 + ncol].rearrange("(k p) n -> p k n", p=128), wb.buf)

    def plan_ffn(l, which):
        g, u, d = which + "_w_gate", which + "_w_up", which + "_w_down"
        for half in range(2):
            for jc in range(4):
                c0 = 2048 * half + 512 * jc
                wq_list.append((which + "g", bview(g, l, 0, D, c0, 512), (8, 512)))
                wq_list.append((which + "u", bview(u, l, 0, D, c0, 512), (8, 512)))
            for oc in range(4):
                wq_list.append((which + "d", bview(d, l, 2048 * half, 2048, 256 * oc, 256), (16, 256)))

    def plan_layer(l):
        plan_ffn(l, "ffn1")
        for c in range(4):
            wq_list.append(("win", bview("w_in", l, 0, D, 512 * c, 512), (8, 512)))
        for c in range(2):
            wq_list.append(("wout", bview("w_out", l, 0, D, 512 * c, 512), (8, 512)))
        wq_list.append(("wq", bview("xattn_wq", l, 0, D, 0, 512), (8, 512)))
        wq_list.append(("wo", bview("xattn_wo", l, 0, 512, 0, D), (4, 1024)))
        plan_ffn(l, "ffn2")

    for l in range(L):
        wq_list.append(("wk", bview("xattn_wk", l, 0, D, 0, 512), (8, 512)))
        wq_list.append(("wv", bview("xattn_wv", l, 0, D, 0, 512), (8, 512)))
    tiles = [("p", i) for i in range(NT)] + [("s", 0)]
    for _t in tiles:
        for l in range(L):
            plan_layer(l)

    for dst, src in [(ident_f, c_ident), (alibi, c_alibi), (diag, c_diag), (sel, c_sel), (misc, c_misc),
                     (gains_sb, gains), (fin_sb, fin_g), (colp_sb, colp), (convw_sb, convw), (bs_sb, bs_bc), (bs_sb_s, bs_bc_s),
                     (dn_sb, dn), (clam_sb, c_lam)]:
        P.DMA("pool", dst, src)
    P.DMA("pool", ident_b, c_ident)
    P.DMA("pool", tril_b, c_tril)
    P.MEMSET(ones_b, 1.0)
    P.MEMSET(onesf64, 1.0 / 64.0)
    P.MEMSET(onesf128, 1.0 / 256.0)
    P.MEMSET(VAh[:, :, :, 64:65], 1.0)
    P.MEMSET(VAown_p[:, :, :, 64:65], 1.0)
    for l in range(L):
        P.MEMSET(sst[l], 0.0)
        P.MEMSET(ctail[l], 0.0)

    def rmsnorm_to_h(N, gcolT):
        sq = A.alloc(8 * 512, BF16, name="sq").re("p (k n) -> p k n", k=8)
        for k in range(8):
            P.ACT(sq[:, k, 0:N], x[:, k, 0:N], AF.Square)
        acc = ps(6, cols=N)
        for k in range(8):
            P.MM(acc, ones_b, sq[:, k, 0:N], start=(k == 0), stop=(k == 7))
        rs = A.alloc(512, name="rs")
        P.ACT(rs[:, 0:N], acc, AF.Sqrt, bias=EPS, scale=1.0 / D)
        P.RECIP(rs[:, 0:N], rs[:, 0:N])
        for k in range(8):
            P.STT(h[:, k, 0:N], x[:, k, 0:N], gcolT[:, k:k + 1], rs[:, 0:N], ALU.mult, ALU.mult)

    def ffn(l, which, N, gidx):
        A.reset()
        rmsnorm_to_h(N, gains_sb[:, l, gidx, :])
        hid = A.alloc(16 * 512, BF16, name="hid").re("p (k n) -> p k n", k=16)
        sg = [A.alloc(512, name="sg%d" % i) for i in range(2)]
        cnt = 0
        for half in range(2):
            for jc in range(4):
                wg = w_next(which + "g")
                wu = w_next(which + "u")
                for m in range(4):
                    G = ps(cnt % 2, cols=N)
                    U = ps(2 + cnt % 2, cols=N)
                    for k in range(8):
                        P.MM(G, wg[:, k, 128 * m:128 * m + 128], h[:, k, 0:N], start=(k == 0), stop=(k == 7))
                    for k in range(8):
                        P.MM(U, wu[:, k, 128 * m:128 * m + 128], h[:, k, 0:N], start=(k == 0), stop=(k == 7))
                    s_ = sg[cnt % 2]
                    P.ACT(s_[:, 0:N], G, AF.Silu)
                    P.TT(hid[:, 4 * jc + m, 0:N], s_[:, 0:N], U, ALU.mult)
                    cnt += 1
            for oc in range(4):
                wd = w_next(which + "d")
                for mi in range(2):
                    i = 2 * oc + mi
                    O = ps(4 + i % 2, cols=N)
                    for k in range(16):
                        P.MM(O, wd[:, k, 128 * mi:128 * mi + 128], hid[:, k, 0:N], start=(k == 0), stop=(k == 15))
                    P.STT(x[:, i, 0:N], O, 0.5, x[:, i, 0:N], ALU.mult, ALU.add)

    def prep():
        A.reset()
        sc = A.alloc(L * 3 * 16, name="sc").re("p (l a g) -> p l a g", l=L, a=3)
        P.DMA("pool", sc, ssm_sc)
        lq = A.alloc(L * 4 * 32, parts=1, name="lq").re("p (l a d) -> p l a d", l=L, a=4)
        P.DMA("pool", lq, lqk)
        pr = A.alloc(L * 2 * 32, parts=1, name="pr").re("p (l a d) -> p l a d", l=L, a=2)
        dots = A.alloc(L * 2, parts=1, name="dots")
        for l in range(L):
            for a in range(2):
                P.TT(pr[:, l, a, :], lq[:, l, 2 * a, :], lq[:, l, 2 * a + 1, :], ALU.mult)
                S.op("dve", (lambda l=l, a=a: (lambda e: e.reduce_sum(out=dots.ap[:, 2 * l + a:2 * l + a + 1], in_=pr.ap[:, l, a, :], axis=mybir.AxisListType.X)))(),
                     reads=[pr.buf], writes=[dots.buf])
        ex = A.alloc(L * 2, parts=1, name="ex")
        P.ACT(ex, dots, AF.Exp)
        lam1 = A.alloc(L, parts=1, name="lam1")
        exv = ex.re("p (l a) -> p l a", a=2)
        P.TT(lam1, exv[:, :, 0], exv[:, :, 1], ALU.subtract)
        ones1 = A.alloc(64, parts=1, name="ones1")
        P.MEMSET(ones1, 1.0)
        pl = ps(7, parts=64, cols=L)
        P.MM(pl, ones1, lam1)
        P.TT(lamneg, pl, clam_sb[:, :, 0], ALU.add)
        P.TS(lamneg, lamneg, -1.0)
        P.TT(gcol, dn_sb, clam_sb[:, :, 1], ALU.mult)
        for l in range(L):
            dt = A.alloc(16, name="dt")
            P.ACT(dt, sc[:, l, 2, :], AF.Exp)
            ard = A.alloc(16, name="ard")
            th = A.alloc(16, name="th")
            P.TT(ard, sc[:, l, 0, :], dt, ALU.mult)
            P.TT(th, sc[:, l, 1, :], dt, ALU.mult)
            r = A.alloc(16, name="r")
            P.ACT(r, ard, AF.Exp)
            tmp = A.alloc(16, name="tmp")
            sn = A.alloc(16, name="sn")
            cs = A.alloc(16, name="cs")
            tmi = A.alloc(16, name="tmi")
            tmk = A.alloc(16, name="tmk")
            sincos(sn, th, tmp, tmi, tmk, 8.0)
            sincos(cs, th, tmp, tmi, tmk, 8.25)
            nr = A.alloc(16, name="nr")
            ni = A.alloc(16, name="ni")
            P.TT(nr, r, cs, ALU.mult)
            P.TS(nr, nr, -1.0, None, ALU.add)
            P.TT(ni, r, sn, ALU.mult)
            den = A.alloc(16, name="den")
            t2 = A.alloc(16, name="t2")
            P.TT(den, sc[:, l, 0, :], sc[:, l, 0, :], ALU.mult)
            P.TT(t2, sc[:, l, 1, :], sc[:, l, 1, :], ALU.mult)
            P.TT(den, den, t2, ALU.add)
            P.RECIP(den, den)
            kr = A.alloc(16, name="kr")
            ki = A.alloc(16, name="ki")
            P.TT(kr, nr, sc[:, l, 0, :], ALU.mult)
            P.TT(t2, ni, sc[:, l, 1, :], ALU.mult)
            P.TT(kr, kr, t2, ALU.add)
            P.TT(kr, kr, den, ALU.mult)
            P.TT(ki, ni, sc[:, l, 0, :], ALU.mult)
            P.TT(t2, nr, sc[:, l, 1, :], ALU.mult)
            P.TT(ki, ki, t2, ALU.subtract)
            P.TT(ki, ki, den, ALU.mult)
            PH = A.alloc(1024, name="PH").re("p (g j) -> p g j", g=16)
            SN = A.alloc(1024, name="SN").re("p (g j) -> p g j", g=16)
            CS = A.alloc(1024, name="CS").re("p (g j) -> p g j", g=16)
            TMP = A.alloc(1024, name="TMP").re("p (g j) -> p g j", g=16)
            TB = A.alloc(5 * 1024, name="TB").re("p (a g j) -> p a g j", a=5, g=16)
            for g in range(16):
                P.TS(PH[:, g, :], misc[:, 0:64], th[:, g:g + 1])
            TMI = A.alloc(1024, name="TMI").re("p (g j) -> p g j", g=16)
            TMK = A.alloc(1024, name="TMK").re("p (g j) -> p g j", g=16)
            sincos(SN, PH, TMP, TMI, TMK, 8.0)
            sincos(CS, PH, TMP, TMI, TMK, 8.25)
            sgn = misc[:, 64:65]
            for g in range(16):
                P.TS(TB[:, 0, g, :], CS[:, g, :], kr[:, g:g + 1])
                P.STT(TB[:, 0, g, :], SN[:, g, :], ki[:, g:g + 1], TB[:, 0, g, :], ALU.mult, ALU.add)
                P.TS(TB[:, 1, g, :], CS[:, g, :], ki[:, g:g + 1])
                P.TS(TMP[:, g, :], SN[:, g, :], kr[:, g:g + 1])
                P.TS(TB[:, 4, g, :], misc[:, 0:64], 0.0, r[:, g:g + 1], ALU.mult, ALU.add)
            P.TT(TB[:, 1], TB[:, 1], TMP, ALU.subtract)
            P.TS(TB[:, 1], TB[:, 1], sgn)
            P.TS(TB[:, 2], CS, sgn, -1.0, ALU.mult, ALU.mult)
            P.TS(TB[:, 3], SN, -1.0)
            P.CP(Etab[:, l, 0, :], CS[:, :, 63])
            P.TS(Etab[:, l, 1, :], SN[:, :, 63], sgn)
            P.DMA("pool", T(s5tab.ap[l], s5tab.buf), TB.re("p a g j -> p (a g j)"))
            A.off -= (16 * 15 + 11 * 1024)
            S.barrier()
        A.reset()
        memf = A.alloc(8 * 256, name="memf").re("p (k n) -> p k n", k=8)
        P.DMA("pool", memf, T(memT.ap.rearrange("(k p) n -> p k n", p=128), memT.buf))
        sq = A.alloc(8 * 256, BF16, name="msq").re("p (k n) -> p k n", k=8)
        for k in range(8):
            P.ACT(sq[:, k, :], memf[:, k, :], AF.Square)
        acc = ps(6, cols=256)
        for k in range(8):
            P.MM(acc, ones_b, sq[:, k, :], start=(k == 0), stop=(k == 7))
        rs = A.alloc(256, name="mrs")
        P.ACT(rs, acc, AF.Sqrt, bias=EPS, scale=1.0 / D)
        P.RECIP(rs, rs)
        mh = A.alloc(8 * 256, BF16, name="mh").re("p (k n) -> p k n", k=8)
        stg = A.alloc(1024, name="stg")
        for l in range(L):
            for k in range(8):
                P.STT(mh[:, k, :], memf[:, k, :], gains_sb[:, l, 3, k:k + 1], rs, ALU.mult, ALU.mult)
            wk = w_next("wk")
            wv = w_next("wv")
            for m in range(4):
                o = ps(m % 2, cols=256)
                for k in range(8):
                    P.MM(o, wk[:, k, 128 * m:128 * m + 128], mh[:, k, :], start=(k == 0), stop=(k == 7))
                st_ = stg[:, 256 * (m % 2):256 * (m % 2) + 256]
                P.ACT(st_, o, AF.Copy)
                P.DMA("pool", T(o_mkT.ap[l, 128 * m:128 * m + 128, :], o_mkT.buf), st_)
            for c in range(2):
                o = ps(2 + c, cols=512)
                for k in range(8):
                    P.MM(o, mh[:, k, 128 * c:128 * c + 128], wv[:, k, :], start=(k == 0), stop=(k == 7))
                st_ = stg[:, 512 * c:512 * c + 512]
                P.ACT(st_, o, AF.Copy)
                P.DMA("pool", T(o_mv.ap[l, 128 * c:128 * c + 128, :], o_mv.buf), st_)

    def sincos(dst, phi, tmp, tmi, tmk, shift):
        I32 = mybir.dt.int32
        P.TS(tmp, phi, 1.0 / (2 * math.pi), shift, ALU.mult, ALU.add)
        ti_ = T(tmi.ap.bitcast(I32), tmi.buf)
        P.CP(ti_, tmp, real=True)
        P.CP(tmk, ti_, real=True)
        P.TT(tmp, tmp, tmk, ALU.subtract)
        P.TS(tmk, tmp, 0.5, None, ALU.is_gt)
        P.TT(tmp, tmp, tmk, ALU.subtract)
        P.ACT(dst, tmp, AF.Sin, scale=2 * math.pi)

    def mixer(l, kind, ti):
        N = 512 if kind == "p" else NS
        t0 = 512 * ti
        nsub = N // 64
        A.reset()
        rmsnorm_to_h(N, gains_sb[:, l, 1, :])
        cp = colp_sb[:, l]
        kstage = A.alloc(2 * 512, name="kstage").re("p (k n) -> p k n", k=2)
        vstage = A.alloc(4 * 256, name="vstage").re("p (c n) -> p c n", c=4)
        gvt = A.alloc(4 * 256, name="gvt").re("p (c n) -> p c n", c=4)
        if kind == "s":
            KTown, VAown = KTown_p, VAown_p
        if kind == "p" and ti > 0:
            P.DMA("pool", KTh[:, :, 0:t0], T(kth_d.ap[l, :, :, 0:t0], kth_d.buf))
            P.DMA("pool", VAh[:, 0:4 * ti].re("p j h d -> p j (h d)"), T(vah_d.ap[l, :, 0:4 * ti, :], vah_d.buf))
        zsig = A.alloc(2 * 512, name="zsig").re("p (k n) -> p k n", k=2)
        w = w_next("win")
        for m in range(4):
            o = ps(m % 2, cols=N)
            for k in range(8):
                P.MM(o, w[:, k, 128 * m:128 * m + 128], h[:, k, 0:N], start=(k == 0), stop=(k == 7))
            if m < 2:
                P.ACT(u_f[:, m, 0:N], o, AF.Copy)
                P.CP(u_b[:, m, 0:N], u_f[:, m, 0:N], eng="pool")
            else:
                P.TS(QT[:, m - 2, 0:N], o, 32.0 ** -0.5)
        phase_done()
        w = w_next("win")
        for m in range(2):
            o = ps(m % 2, cols=N)
            for k in range(8):
                P.MM(o, w[:, k, 128 * m:128 * m + 128], h[:, k, 0:N], start=(k == 0), stop=(k == 7))
            P.ACT(kstage[:, m, 0:N], o, AF.Copy)
            if kind == "p":
                P.CP(KTh[:, m, t0:t0 + N], kstage[:, m, 0:N], eng="pool")
                P.DMA("pool", T(o_kT_p.ap[l, 128 * m:128 * m + 128, t0:t0 + N], o_kT_p.buf), kstage[:, m, 0:N])
            else:
                P.CP(KTown[:, m, :], kstage[:, m, 0:N], eng="pool")
                P.DMA("pool", T(o_kT_s.ap[l, 128 * m:128 * m + 128, :], o_kT_s.buf), kstage[:, m, 0:N])
        if kind == "p":
            for c in range(4):
                o = ps(2 + c % 2, cols=256)
                for k in range(8):
                    P.MM(o, h[:, k, 128 * c:128 * c + 128], w[:, k, 256:512], start=(k == 0), stop=(k == 7))
                P.ACT(vstage[:, c, :], o, AF.Copy)
                P.CP(VAh[:, 4 * ti + c, :, 0:64], vstage[:, c, :].re("p (h d) -> p h d", h=4), eng="pool")
            P.DMA("pool", T(o_v_p.ap[l, t0:t0 + 512, :].rearrange("(c p) n -> p c n", p=128), o_v_p.buf), vstage)
            if ti < NT - 1:
                P.DMA("pool", T(kth_d.ap[l, :, :, t0:t0 + 512], kth_d.buf), KTh[:, :, t0:t0 + 512])
                P.DMA("pool", T(vah_d.ap[l, :, 4 * ti:4 * ti + 4, :], vah_d.buf), VAh[:, 4 * ti:4 * ti + 4].re("p j h d -> p j (h d)"))
        else:
            for s_ in range(NSS):
                o = ps(2 + s_ % 2, parts=64, cols=256)
                for k in range(8):
                    P.MM(o, h[:, k, 64 * s_:64 * s_ + 64], w[:, k, 256:512], start=(k == 0), stop=(k == 7))
                P.ACT(vstage[0:64, s_, :], o, AF.Copy)
                P.CP(VAown[:, s_, :, 0:64], vstage[0:64, s_, :].re("p (h d) -> p h d", h=4), eng="pool")
            P.DMA("pool", T(o_v_s.ap[l].rearrange("(s p) n -> p s n", p=64), o_v_s.buf), vstage[0:64])
        phase_done()
        w = w_next("win")
        for m in range(2):
            og = ps(m % 2, cols=N)
            for k in range(8):
                P.MM(og, w[:, k, 256 + 128 * m:256 + 128 * m + 128], h[:, k, 0:N], start=(k == 0), stop=(k == 7))
            oz = ps(2 + m % 2, cols=N)
            for k in range(8):
                P.MM(oz, w[:, k, 128 * m:128 * m + 128], h[:, k, 0:N], start=(k == 0), stop=(k == 7))
            P.ACT(zsig[:, m, 0:N], og, AF.Sigmoid)
            P.TT(zsig[:, m, 0:N], zsig[:, m, 0:N], oz, ALU.mult)
        phase_done()
        w = w_next("win")
        for m in range(2):
            o = ps(m % 2, cols=N)
            for k in range(8):
                P.MM(o, w[:, k, 128 * m:128 * m + 128], h[:, k, 0:N], start=(k == 0), stop=(k == 7))
            P.ACT(gu[:, m, 0:N], o, AF.Gelu_apprx_tanh)
        nch = N // 128
        for c in range(nch):
            o = ps(2 + c % 2, cols=256)
            for k in range(8):
                P.MM(o, h[:, k, 128 * c:128 * c + 128], w[:, k, 256:512], start=(k == 0), stop=(k == 7))
            P.ACT(gvt[:, c, :], o, AF.Gelu_apprx_tanh)

        phase_done()
        st6 = A.alloc(nch * 6, name="st6").re("p (c s) -> p c s", c=nch)
        mv2 = A.alloc(nch * 2, name="mv2").re("p (c s) -> p c s", c=nch)
        rstd = A.alloc(nch, name="grstd")
        vb = A.alloc(nch * 256, BF16, name="vb").re("p (c n) -> p c n", c=nch)
        wsb = A.alloc(512, BF16, name="wsb").re("p (h i) -> p h i", h=4)
        wsr = A.alloc(512, BF16, name="wsr").re("p (h i) -> p h i", h=4)
        if kind == "p":
            P.DMA("pool", wsr, T(wsT.ap[l], wsT.buf))
            bsv = bs_sb
        else:
            P.MEMSET(wsr, 0.0)
            P.DMA("pool", wsr[0:64, :, 0:64], T(wsT.ap[l, 0:64, :, 0:64], wsT.buf))
            P.DMA("pool", wsr[64:128, :, 64:128], T(wsT.ap[l, 0:64, :, 0:64], wsT.buf))
            bsv = bs_sb_s
        for hh in range(4):
            P.TT(wsb[:, hh, :], wsr[:, hh, :], tril_b, ALU.mult)
        for c in range(nch):
            S.op("dve", (lambda c=c: (lambda e: e.bn_stats(out=st6.ap[:, c, :], in_=gvt.ap[:, c, :])))(), reads=[gvt.buf], writes=[st6.buf])
            S.op("dve", (lambda c=c: (lambda e: e.bn_aggr(out=mv2.ap[:, c, :], in_=st6.ap[:, c, :])))(), reads=[st6.buf], writes=[mv2.buf])
        P.ACT(rstd, mv2[:, :, 1], AF.Sqrt, bias=EPS)
        P.RECIP(rstd, rstd)
        for c in range(nch):
            P.TS(gvt[:, c, :], gvt[:, c, :], mv2[:, c, 0:1], rstd[:, c:c + 1], ALU.subtract, ALU.mult)
            P.TT(gvt[:, c, :], gvt[:, c, :], T(gm_sb.ap[:, l, 0, :], gm_sb.buf), ALU.mult)
            P.TT(gvt[:, c, :], gvt[:, c, :], T(gm_sb.ap[:, l, 1, :], gm_sb.buf), ALU.add)
            P.CP(vb[:, c, :], gvt[:, c, :], eng="pool")
        if kind == "s":
            P.DMA("pool", T(o_gv_s.ap[l].rearrange("(c p) n -> p c n", p=128), o_gv_s.buf), gvt[:, 0:nch, :])
        tmpm = A.alloc(128, name="tmpm")
        for c in range(nch):
            for m in range(2):
                bk = 4 + (2 * c + m) % 2
                for hh in range(2):
                    hd = 2 * m + hh
                    P.MM(T(PS[:, bk, 128 * hh:128 * hh + 128], psb[bk]), vb[:, c, 128 * m:128 * m + 128], wsb[:, hd, :])
                for hh in range(2):
                    rows = slice(64 * hh, 64 * hh + 64)
                    P.TT(tmpm[rows, :], T(PS[rows, bk, 128 * hh:128 * hh + 128], psb[bk]), bsv[rows, l, m, :], ALU.add)
                P.TT(mix[:, 6 + m, 128 * c:128 * c + 128], tmpm, gu[:, m, 128 * c:128 * c + 128], ALU.mult)

        phase_done()
        dg = A.alloc(31 * 128, BF16, name="dg").re("p (k n) -> p k n", k=31)
        cy = A.alloc(2 * 512, name="cy").re("p (k n) -> p k n", k=2)
        if kind == "p":
            zv = zp[:, :, 0:542]
            for m in range(2):
                P.CP(zv[:, m, 0:30], ctail[l][:, m, :], eng="pool")
                P.CP(zv[:, m, 30:542], zsig[:, m, :])
                P.CP(ctail[l][:, m, :], zv[:, m, 512:542], eng="pool")
            if ti == NT - 1:
                for m in range(2):
                    P.DMA("pool", T(o_conv_p.ap[l, 128 * m:128 * m + 128, :], o_conv_p.buf), zsig[:, m, 482:512])
        else:
            zv4 = zp[:, :, 0:NSS * 94].re("p k (s n) -> p k s n", s=NSS)
            cbf = A.alloc(2 * NSS * 30, name="cbf").re("p (k s n) -> p k s n", k=2, s=NSS)
            P.DMA("pool", cbf, T(cbuf.ap[:, l], cbuf.buf))
            for m in range(2):
                P.CP(zv4[:, m, :, 0:30], cbf[:, m], eng="pool")
                P.CP(zv4[:, m, :, 30:94], zsig[:, m, 0:NS].re("p (s n) -> p s n", s=NSS))
                P.DMA("pool", T(o_conv_s.ap[l, :, 128 * m:128 * m + 128, :].rearrange("s p n -> p s n"), o_conv_s.buf),
                      zsig[:, m, 0:NS].re("p (s n) -> p s n", s=NSS)[:, :, 34:64])
        for m in range(2):
            for k in range(31):
                P.TS(dg[:, k, :], ident_b, T(convw_sb.ap[:, l, m, k:k + 1], convw_sb.buf))
            o = ps(m % 2, cols=N)
            for k in range(31):
                if kind == "p":
                    rhs = zv[:, m, k:k + 512]
                    P.MM(o, dg[:, k, :], rhs, start=(k == 0), stop=(k == 30))
                else:
                    rhs = zv4[:, m, :, k:k + 64]
                    P.MM(T(PS[:, m % 2, 0:NS].rearrange("p (s n) -> p s n", s=NSS), o.buf), dg[:, k, :], rhs, start=(k == 0), stop=(k == 30))
            P.ACT(cy[:, m, 0:N], o, AF.Identity, bias=cp[:, 2, m:m + 1])
        ln_feature_major(cy, N, cp[:, 3, :], cp[:, 4, :])
        cyb = A.alloc(2 * 512, BF16, name="cyb").re("p (k n) -> p k n", k=2)
        for m in range(2):
            P.ACT(cyb[:, m, 0:N], cy[:, m, 0:N], AF.Silu)
        wpw = A.alloc(2 * 256, BF16, name="wpw").re("p (k n) -> p k n", k=2)
        P.DMA("pool", wpw, kview(Wn["conv_w_pw"], l, 0, 256, 0, 256))
        for m in range(2):
            o = ps(2 + m % 2, cols=N)
            for k in range(2):
                P.MM(o, wpw[:, k, 128 * m:128 * m + 128], cyb[:, k, 0:N], start=(k == 0), stop=(k == 1))
            P.ACT(mix[:, 4 + m, 0:N], o, AF.Copy)

        phase_done()
        s5(l, kind, ti, N, nsub, cp)
        phase_done()

        attention(l, kind, ti, N, KTown if kind == "s" else None, VAown if kind == "s" else None)
        phase_done()

        A.reset()
        for c in range(2):
            w = w_next("wout")
            for mi in range(4):
                i = 4 * c + mi
                o = ps(i % 2, cols=N)
                for k in range(8):
                    P.MM(o, w[:, k, 128 * mi:128 * mi + 128], mix[:, k, 0:N], start=(k == 0), stop=(k == 7))
                P.TT(x[:, i, 0:N], x[:, i, 0:N], o, ALU.add)

    gm_sb = P.sb("gm_sb", [128, L, 2, 256])
    P.DMA("pool", gm_sb, gm_gb)

    def ln_feature_major(v, N, gT, bT):
        sq = A.alloc(2 * 512, name="lnsq").re("p (k n) -> p k n", k=2)
        for m in range(2):
            P.ACT(sq[:, m, 0:N], v[:, m, 0:N], AF.Square)
        mu = ps(4, cols=N)
        e2 = ps(5, cols=N)
        for m in range(2):
            P.MM(mu, onesf128, v[:, m, 0:N], start=(m == 0), stop=(m == 1))
        for m in range(2):
            P.MM(e2, onesf128, sq[:, m, 0:N], start=(m == 0), stop=(m == 1))
        var = A.alloc(512, name="lnvar")
        mus = A.alloc(512, name="lnmu")
        P.CP(mus[:, 0:N], mu)
        P.TT(var[:, 0:N], mus[:, 0:N], mus[:, 0:N], ALU.mult)
        P.TT(var[:, 0:N], e2, var[:, 0:N], ALU.subtract)
        P.ACT(var[:, 0:N], var[:, 0:N], AF.Sqrt, bias=EPS)
        P.RECIP(var[:, 0:N], var[:, 0:N])
        for m in range(2):
            P.TT(v[:, m, 0:N], v[:, m, 0:N], mus[:, 0:N], ALU.subtract)
            P.TT(v[:, m, 0:N], v[:, m, 0:N], var[:, 0:N], ALU.mult)
            P.TS(v[:, m, 0:N], v[:, m, 0:N], gT[:, m:m + 1], bT[:, m:m + 1], ALU.mult, ALU.add)

    def s5(l, kind, ti, N, nsub, cp):
        A.reset()
        TB = A.alloc(5 * 1024, name="TBl").re("p (a g j) -> p a g j", a=5, g=16)
        P.DMA("pool", TB.re("p a g j -> p (a g j)"), T(s5tab.ap[l], s5tab.buf))
        Bw = A.alloc(2 * 16 * 128, BF16, name="Bw").re("p (a g n) -> p a g n", a=2, g=16)
        Cw = A.alloc(2 * 16 * 128, BF16, name="Cw").re("p (a g n) -> p a g n", a=2, g=16)
        P.DMA("pool", Bw[:, 0], T(Bx_d.ap[l], Bx_d.buf))
        P.DMA("pool", Bw[:, 1], T(Bxs_d.ap[l], Bxs_d.buf))
        P.DMA("pool", Cw[:, 0], T(Cx1_d.ap[l], Cx1_d.buf))
        P.DMA("pool", Cw[:, 1], T(Cx2_d.ap[l], Cx2_d.buf))
        wgl = A.alloc(2 * 256, BF16, name="wgl").re("p (k n) -> p k n", k=2)
        P.DMA("pool", wgl, kview(Wn["ssm_w_glu"], l, 0, 256, 0, 256))
        s5mark = A.off
        bp = A.alloc(1024, name="bp").re("p (g j) -> p g j", g=16)
        t1 = A.alloc(1024, name="t1").re("p (g j) -> p g j", g=16)
        Sp = A.alloc(1024, name="Sp").re("p (g j) -> p g j", g=16)
        P12 = A.alloc(2 * 1024, BF16, name="P12").re("p (a g j) -> p a g j", a=2, g=16)
        init = A.alloc(16, name="init")
        sw = A.alloc(16, name="sw")
        t16 = A.alloc(16, name="t16")
        if kind == "s":
            s0sb = A.alloc(NSS * 16, name="s0sb").re("p (s g) -> p s g", s=NSS)
            P.DMA("pool", s0sb, T(s0.ap[:, l], s0.buf))
            send = A.alloc(NSS * 16, name="send").re("p (s g) -> p s g", s=NSS)
        ybuf = [Buf("yacc0", psum=True), Buf("yacc1", psum=True)]
        for sI in range(nsub):
            c0 = 64 * sI
            for hf in range(2):
                for g8 in range(8):
                    g = 8 * hf + g8
                    P.MM(T(PS[:, hf, 64 * g8:64 * g8 + 64], psb[hf]), Bw[:, 0, g, :], u_b[:, g // 8, c0:c0 + 64])
                for g8 in range(8):
                    g = 8 * hf + g8
                    P.MM(T(PS[:, 2 + hf, 64 * g8:64 * g8 + 64], psb[2 + hf]), Bw[:, 1, g, :], u_b[:, g // 8, c0:c0 + 64])
            for hf in range(2):
                gs = slice(8 * hf, 8 * hf + 8)
                Xv = T(PS[:, hf, :].rearrange("p (g j) -> p g j", g=8), psb[hf])
                Xsv = T(PS[:, 2 + hf, :].rearrange("p (g j) -> p g j", g=8), psb[2 + hf])
                P.TT(bp[:, gs, :], Xv, TB[:, 0, gs, :], ALU.mult)
                P.TT(t1[:, gs, :], Xsv, TB[:, 1, gs, :], ALU.mult)
            P.TT(bp, bp, t1, ALU.add)
            if kind == "p":
                ini = sst[l]
            else:
                ini = s0sb[:, sI, :]
            for g in range(16):
                P.SCAN(Sp[:, g, :], TB[:, 4, g, :], bp[:, g, :], ini[:, g:g + 1])
            last = Sp[:, :, 63]
            P.CP(sw[0:64, :], last[64:128, :])
            P.CP(sw[64:128, :], last[0:64, :])
            P.TT(t16, last, Etab[:, l, 0, :], ALU.mult)
            P.TT(sw, sw, Etab[:, l, 1, :], ALU.mult)
            if kind == "p":
                P.TT(sst[l], t16, sw, ALU.add)
            else:
                P.TT(send[:, sI, :], t16, sw, ALU.add)
            P.TT(P12[:, 0], Sp, TB[:, 2], ALU.mult)
            P.TT(P12[:, 1], Sp, TB[:, 3], ALU.mult, eng="pool")
            for m in range(2):
                yo = T(PS[:, 4 + m, c0:c0 + 64], ybuf[m])
                n = 0
                for g8 in range(8):
                    g = 8 * m + g8
                    for a in range(2):
                        P.MM(yo, Cw[:, a, g, :], P12[:, a, g, :], start=(n == 0), stop=(n == 15))
                        n += 1
        if kind == "p":
            if ti == NT - 1:
                P.DMA("pool", T(o_ssm_p.ap[l], o_ssm_p.buf), sst[l])
        else:
            P.DMA("pool", T(o_ssm_s.ap[l].rearrange("s p g -> p s g"), o_ssm_s.buf), send)
        S.barrier()
        A.off = s5mark
        gf = A.alloc(2 * 512, name="gf").re("p (k n) -> p k n", k=2)
        gb = A.alloc(2 * 512, BF16, name="gb").re("p (k n) -> p k n", k=2)
        for m in range(2):
            yo = T(PS[:, 4 + m, 0:N], ybuf[m])
            P.STT(gf[:, m, 0:N], u_f[:, m, 0:N], cp[:, 0, m:m + 1], yo, ALU.mult, ALU.add)
            P.ACT(gf[:, m, 0:N], gf[:, m, 0:N], AF.Gelu_apprx_tanh)
            P.CP(gb[:, m, 0:N], gf[:, m, 0:N], eng="pool")
        for m in range(2):
            o = ps(6 + m % 2, cols=N)
            for k in range(2):
                P.MM(o, wgl[:, k, 128 * m:128 * m + 128], gb[:, k, 0:N], start=(k == 0), stop=(k == 1))
            sgm = A.alloc(512, name="sgm%d" % m)
            P.ACT(sgm[:, 0:N], o, AF.Sigmoid, bias=cp[:, 1, m:m + 1])
            P.TT(mix[:, m, 0:N], gf[:, m, 0:N], sgm[:, 0:N], ALU.mult)

    def attention(l, kind, ti, N, KTown, VAown):
        A.reset()
        nq = 4 if kind == "p" else NSS
        QN = 128 if kind == "p" else 64
        accS = A.alloc(2 * 4 * N, parts=65, name="accS").re("p (a h q) -> p a h q", a=2, h=4)
        NSLOT = 4
        PT = A.alloc(NSLOT * 4 * 128, BF16, name="PT").re("p (s r q) -> p s r q", s=NSLOT, r=4)
        sbuf_ = [[Buf("sc_%d" % hh, psum=True) for hh in range(2)]] * NSLOT
        ptb = [[Buf("pt%d_%d" % (s_, hh)) for hh in range(2)] for s_ in range(NSLOT)]
        accb = [Buf("acc0", psum=True), Buf("acc1", psum=True)]
        if kind == "s":
            KTc = A.alloc(2 * PAST, BF16, name="KTc").re("p (k n) -> p k n", k=2)
            VAc = A.alloc(16 * 4 * 65, BF16, name="VAc").re("p (j h d) -> p j h d", j=16, h=4)
            P.MEMSET(VAc[:, :, :, 64:65], 1.0)
        step = 0
        scb = [Buf("scs0", psum=True), Buf("scs1", psum=True)]
        ptb2 = [Buf("pts%d" % i_) for i_ in range(NSLOT)]
        for qi in range(nq):
            if kind == "p":
                Q = 4 * ti + qi
                keys = [("h", j, Q - j) for j in range(Q + 1)]
            else:
                P.DMA("pool", KTc, T(ckT.ap[l, qi].rearrange("(k p) n -> p k n", p=128), ckT.buf))
                for hd_ in range(4):
                    P.DMA("pool", VAc[:, :, hd_, 0:64], T(cv.ap[l, qi].rearrange("(j p) (h d) -> p j h d", p=128, h=4)[:, :, hd_, :], cv.buf))
                keys = [("c", j, 16 - j) for j in range(16)] + [("o", qi, 0)]
            for kt in range(2):
                ab = qi * 2 + kt
                accbank = 4 + ab % 2
                pend = None
                nfirst = [True]
                for si, (src, j, d) in enumerate(keys):
                    KP = 128 if src != "o" else 64
                    for hh in range(2):
                        slot = step % 2
                        pslot = step % NSLOT
                        step += 1
                        hd = 2 * kt + hh
                        for a in range(2):
                            r = 2 * hh + a
                            if src == "h":
                                lhs = KTh[32 * r:32 * r + 32, kt, 128 * j:128 * j + 128]
                            elif src == "c":
                                lhs = KTc[32 * r:32 * r + 32, kt, 128 * j:128 * j + 128]
                            else:
                                lhs = KTown[32 * r:32 * r + 32, kt, 64 * j:64 * j + 64]
                            rhs = QT[32 * r:32 * r + 32, kt, QN * qi:QN * qi + QN]
                            P.MM(T(PS[0:KP, 2 * slot + a, 0:QN], scb[slot]), lhs, rhs, tile_position=(32 * r, 0))
                        if pend is not None:
                            pend()
                        sv = T(PS[0:KP, 2 * slot:2 * slot + 2, 0:QN], scb[slot])
                        pv = T(PT.ap[0:KP, pslot, 2 * hh:2 * hh + 2, 0:QN], ptb2[pslot])
                        if d == 0:
                            for a in range(2):
                                P.TT(T(PS[0:KP, 2 * slot + a, 0:QN], scb[slot]), T(PS[0:KP, 2 * slot + a, 0:QN], scb[slot]),
                                     diag[0:KP, hd, 0:QN], ALU.add)
                            P.ACT(pv, sv, AF.Exp)
                        else:
                            P.ACT(pv, sv, AF.Exp, bias=alibi[0:KP, hd, d:d + 1])

                        def av(pslot=pslot, src=src, j=j, KP=KP, hh=hh, hd=hd, accbank=accbank, ab=ab):
                            for a in range(2):
                                r = 2 * hh + a
                                if src == "h":
                                    lhs = VAh[:, j, hd, :]
                                elif src == "c":
                                    lhs = VAc[:, j, hd, :]
                                else:
                                    lhs = VAown[:, j, hd, :]
                                st_ = nfirst[0]
                                nfirst[0] = False
                                P.MM(T(PS[0:65, accbank, 128 * r:128 * r + QN], accb[ab % 2]), lhs,
                                     T(PT.ap[0:KP, pslot, r, 0:QN], ptb2[pslot]),
                                     start=st_, stop=True, skip_group_check=True)
                        pend = av
                pend()
                src_v = T(PS[0:65, accbank, :].rearrange("p (hh a q) -> p a hh q", hh=2, a=2)[:, :, :, 0:QN], accb[ab % 2])
                P.ACT(accS[:, :, 2 * kt:2 * kt + 2, QN * qi:QN * qi + QN], src_v, AF.Copy)
        Rz = A.alloc(512, parts=64, name="Rz")
        for a in range(2):
            for hd in range(4):
                zb = ps(6 + (2 * a + hd) % 2, parts=64, cols=N)
                P.MM(zb, sel, accS[:, a, hd, 0:N])
                P.RECIP(Rz[:, 0:N], zb)
                P.TT(accS[0:64, a, hd, 0:N], accS[0:64, a, hd, 0:N], Rz[:, 0:N], ALU.mult)
        Dd = A.alloc(4 * N, parts=64, name="Dd").re("p (h q) -> p h q", h=4)
        Dq = A.alloc(4 * N, parts=64, name="Dq").re("p (h q) -> p h q", h=4)
        P.STT(Dd[:, :, 0:N], accS[0:64, 1, :, 0:N], lamneg[:, l:l + 1], accS[0:64, 0, :, 0:N], ALU.mult, ALU.add)
        P.ACT(Dq[:, :, 0:N], Dd[:, :, 0:N], AF.Square)
        for hd in range(4):
            mo = ps(6 + hd % 2, parts=64, cols=N)
            P.MM(mo, onesf64, Dq[:, hd, 0:N])
            P.ACT(Rz[:, 0:N], mo, AF.Sqrt, bias=EPS)
            P.RECIP(Rz[:, 0:N], Rz[:, 0:N])
            P.TS(Dd[:, hd, 0:N], Dd[:, hd, 0:N], gcol[:, l:l + 1])
            hh = hd % 2
            P.TT(mix[64 * hh:64 * hh + 64, 2 + hd // 2, 0:N], Dd[:, hd, 0:N], Rz[:, 0:N], ALU.mult)

    def xattn(l, kind, N):
        A.reset()
        rmsnorm_to_h(N, gains_sb[:, l, 2, :])
        QX = A.alloc(4 * 512, BF16, name="QX").re("p (h n) -> p h n", h=4)
        ob = A.alloc(4 * 512, BF16, name="ob").re("p (h n) -> p h n", h=4)
        PX = [A.alloc(512, BF16, name="PX%d" % i) for i in range(2)]
        Rz = A.alloc(512, name="xRz")
        w = w_next("wq")
        for m in range(4):
            o = ps(m % 2, cols=N)
            for k in range(8):
                P.MM(o, w[:, k, 128 * m:128 * m + 128], h[:, k, 0:N], start=(k == 0), stop=(k == 7))
            P.TS(QX[:, m, 0:N], o, 128.0 ** -0.5)
        mk_s = A.alloc(4 * 256, BF16, name="mk_s").re("p (h n) -> p h n", h=4)
        mv_s = A.alloc(2 * 512, BF16, name="mv_s").re("p (c n) -> p c n", c=2)
        segs = [(0, N, None)] if kind == "p" else [(64 * s_, 64, s_) for s_ in range(NSS)]
        cnt = 0
        for (q0, qn, sidx) in segs:
            if sidx is None:
                P.DMA("pool", mk_s, T(o_mkT.ap[l].rearrange("(h p) n -> p h n", p=128), o_mkT.buf))
                P.DMA("pool", mv_s, T(o_mv.ap[l].rearrange("(c p) n -> p c n", p=128), o_mv.buf))
            else:
                P.DMA("pool", mk_s, T(cmkT.ap[l, sidx].rearrange("(h p) n -> p h n", p=128), cmkT.buf))
                P.DMA("pool", mv_s, T(cmv.ap[l, sidx].rearrange("(c p) n -> p c n", p=128), cmv.buf))
            mk, mv = mk_s, mv_s
            for hd in range(4):
                oacc = ps(4, cols=qn)
                zacc = ps(5, cols=qn)
                for c in range(2):
                    sc_ = ps(cnt % 2, cols=qn)
                    P.MM(sc_, mk[:, hd, 128 * c:128 * c + 128], QX[:, hd, q0:q0 + qn])
                    px = PX[cnt % 2]
                    cnt += 1
                    P.ACT(px[:, 0:qn], sc_, AF.Exp)
                    P.MM(oacc, mv[:, c, 128 * hd:128 * hd + 128], px[:, 0:qn], start=(c == 0), stop=(c == 1))
                    P.MM(zacc, ones_b, px[:, 0:qn], start=(c == 0), stop=(c == 1))
                P.RECIP(Rz[:, 0:qn], zacc)
                P.TT(ob[:, hd, q0:q0 + qn], oacc, Rz[:, 0:qn], ALU.mult)
        w = w_next("wo")
        for i in range(8):
            o = ps(2 + i % 2, cols=N)
            for k in range(4):
                P.MM(o, w[:, k, 128 * i:128 * i + 128], ob[:, k, 0:N], start=(k == 0), stop=(k == 3))
            P.TT(x[:, i, 0:N], x[:, i, 0:N], o, ALU.add)

    def main_program():
        for l_ in range(L):
            convert_weights(l_, only=["xattn_wk", "xattn_wv"])
        convert_weights(0)
        prep()
        phase_done()
        for (kind, ti) in tiles:
            N = 512 if kind == "p" else NS
            A.reset()
            if kind == "p":
                P.DMA("pool", x, T(xT_p.ap[:, 512 * ti:512 * ti + 512].rearrange("(k p) n -> p k n", p=128), xT_p.buf))
            else:
                P.DMA("pool", x[:, :, 0:N], T(xT_s.ap.rearrange("(k p) n -> p k n", p=128), xT_s.buf))
            for l in range(L):
                if (kind, ti) == tiles[0] and l + 1 < L:
                    convert_weights(l + 1)
                ffn(l, "ffn1", N, 0)
                phase_done()
                mixer(l, kind, ti)
                phase_done()
                xattn(l, kind, N)
                phase_done()
                ffn(l, "ffn2", N, 4)
                phase_done()
            A.reset()
            sq = A.alloc(8 * 512, BF16, name="fsq").re("p (k n) -> p k n", k=8)
            for k in range(8):
                P.ACT(sq[:, k, 0:N], x[:, k, 0:N], AF.Square)
            acc = ps(6, cols=N)
            for k in range(8):
                P.MM(acc, ones_b, sq[:, k, 0:N], start=(k == 0), stop=(k == 7))
            rs = A.alloc(512, name="frs")
            P.ACT(rs[:, 0:N], acc, AF.Sqrt, bias=EPS, scale=1.0 / D)
            P.RECIP(rs[:, 0:N], rs[:, 0:N])
            yo = A.alloc(8 * 512, name="yo").re("p (k n) -> p k n", k=8)
            for k in range(8):
                P.STT(yo[:, k, 0:N], x[:, k, 0:N], fin_sb[:, k:k + 1], rs[:, 0:N], ALU.mult, ALU.mult)
            if kind == "p":
                P.DMA("pool", T(yT_p.ap[:, 512 * ti:512 * ti + 512].rearrange("(k p) n -> p k n", p=128), yT_p.buf), yo)
            else:
                P.DMA("pool", T(yT_s.ap.rearrange("(k p) n -> p k n", p=128), yT_s.buf), yo[:, :, 0:N])
        assert wstate["taken"] == len(wq_list), (wstate, len(wq_list))

    try:
        main_program()
    except StopBuild:
        pass
    S.barrier()
    S.final_wait_all("sp")
    S.emit()
    P.st.close()
    return nc


def _consts(L):
    slopes = 2.0 ** (-8.0 * np.arange(1, 5, dtype=np.float64) / 4)
    kr = np.arange(128)[:, None, None]
    d = np.arange(33)[None, None, :]
    alibi = (slopes[None, :, None] * (kr - 128 * d - 64)).astype(np.float32)
    qr = np.arange(128)[None, None, :]
    vis = (kr // 64) <= (qr // 64)
    dg = -slopes[None, :, None] * np.abs(qr - kr) + slopes[None, :, None] * (qr - 64)
    diag = np.where(vis, dg, -30000.0).astype(np.float32)
    tril = (np.arange(128)[None, :] >= np.arange(128)[:, None]).astype(np.float32)
    sel = np.zeros((65, 64), np.float32)
    sel[64, :] = 1.0
    misc = np.zeros((128, 66), np.float32)
    misc[:, 0:64] = np.arange(1, 65, dtype=np.float32)[None, :]
    misc[0:64, 64] = -1.0
    misc[64:128, 64] = 1.0
    clam = np.zeros((64, L, 2), np.float32)
    for l in range(L):
        li = 0.8 - 0.6 * math.exp(-0.3 * l)
        clam[:, l, 0] = li
        clam[:, l, 1] = 1.0 - li
    return dict(c_ident=np.eye(128, dtype=np.float32), c_alibi=alibi, c_diag=diag, c_tril=tril, c_sel=sel,
                c_misc=misc, c_lam=clam)


def _fm(a):
    Ln, n = a.shape
    return np.ascontiguousarray(a.reshape(Ln, n // 128, 128).transpose(2, 0, 1))


def _shared_inputs(inp, L):
    f = lambda k: np.asarray(inp[k], dtype=np.float32)
    sh = {}
    for nm in ["ffn1_w_gate", "ffn1_w_up", "ffn1_w_down", "ffn2_w_gate", "ffn2_w_up", "ffn2_w_down", "w_in", "w_out",
               "xattn_wq", "xattn_wk", "xattn_wv", "xattn_wo", "ssm_w_glu", "conv_w_pw"]:
        sh[nm] = np.ascontiguousarray(f(nm)[:L])
    g = np.stack([_fm(f(n)[:L]) for n in ["ffn1_norm", "mix_norm", "xattn_norm", "mem_norm", "ffn2_norm"]], axis=2)
    sh["gains"] = np.ascontiguousarray(g)
    sh["fin_g"] = np.ascontiguousarray(f("final_norm").reshape(8, 128).T)
    bre, bim, cre, cim = f("ssm_b_re")[:L], f("ssm_b_im")[:L], f("ssm_c_re")[:L], f("ssm_c_im")[:L]
    Bx = np.zeros((L, 128, 16, 128), np.float32)
    Bxs = np.zeros((L, 128, 16, 128), np.float32)
    Cx1 = np.zeros((L, 128, 16, 128), np.float32)
    Cx2 = np.zeros((L, 128, 16, 128), np.float32)
    for g_ in range(16):
        r0 = 16 * (g_ % 8)
        Bx[:, r0:r0 + 16, g_, 0:64] = bre[:, g_].transpose(0, 2, 1)
        Bx[:, r0:r0 + 16, g_, 64:128] = bim[:, g_].transpose(0, 2, 1)
        Bxs[:, r0:r0 + 16, g_, 0:64] = bim[:, g_].transpose(0, 2, 1)
        Bxs[:, r0:r0 + 16, g_, 64:128] = bre[:, g_].transpose(0, 2, 1)
        Cx1[:, 0:64, g_, r0:r0 + 16] = cre[:, g_].transpose(0, 2, 1)
        Cx1[:, 64:128, g_, r0:r0 + 16] = cim[:, g_].transpose(0, 2, 1)
        Cx2[:, 0:64, g_, r0:r0 + 16] = cim[:, g_].transpose(0, 2, 1)
        Cx2[:, 64:128, g_, r0:r0 + 16] = cre[:, g_].transpose(0, 2, 1)
    sh.update(Bx=Bx, Bxs=Bxs, Cx1=Cx1, Cx2=Cx2)
    sc = np.zeros((128, L, 3, 16), np.float32)
    are, aim, ldt = f("ssm_a_re")[:L], f("ssm_a_im")[:L], f("ssm_log_dt")[:L]
    sc[0:64, :, 0, :] = are.transpose(2, 0, 1)
    sc[64:128, :, 0, :] = are.transpose(2, 0, 1)
    sc[0:64, :, 1, :] = aim.transpose(2, 0, 1)
    sc[64:128, :, 1, :] = aim.transpose(2, 0, 1)
    sc[:, :, 2, :] = ldt[None, :, :]
    sh["ssm_sc"] = sc
    colp = np.zeros((128, L, 7, 2), np.float32)
    colp[:, :, 0, :] = _fm(f("ssm_d")[:L].reshape(L, 256))
    colp[:, :, 1, :] = _fm(f("ssm_b_glu")[:L])
    colp[:, :, 2, :] = _fm(f("conv_b")[:L])
    colp[:, :, 3, :] = _fm(f("conv_ln_g")[:L])
    colp[:, :, 4, :] = _fm(f("conv_ln_b")[:L])
    sh["colp"] = colp
    cw = f("conv_w")[:L]
    sh["convw"] = np.ascontiguousarray(cw.reshape(L, 31, 2, 128).transpose(3, 0, 2, 1))
    gm = np.stack([f("gmlp_ln_g")[:L], f("gmlp_ln_b")[:L]], axis=1)
    sh["gm_gb"] = np.ascontiguousarray(np.broadcast_to(gm[None], (128, L, 2, 256)))
    sh["wsT"] = np.ascontiguousarray(f("gmlp_ws")[:L].transpose(0, 3, 1, 2))
    bs = f("gmlp_bs")[:L]
    bsb = np.zeros((128, L, 2, 128), np.float32)
    for m in range(2):
        for hh in range(2):
            bsb[64 * hh:64 * hh + 64, :, m, :] = bs[:, 2 * m + hh, :][None]
    sh["bs_bc"] = bsb
    bsb2 = np.zeros((128, L, 2, 128), np.float32)
    for m in range(2):
        for hh in range(2):
            bsb2[64 * hh:64 * hh + 64, :, m, 0:64] = bs[:, 2 * m + hh, 0:64][None]
            bsb2[64 * hh:64 * hh + 64, :, m, 64:128] = bs[:, 2 * m + hh, 0:64][None]
    sh["bs_bc_s"] = bsb2
    sh["lqk"] = np.ascontiguousarray(np.stack([f("dattn_lq1")[:L], f("dattn_lk1")[:L], f("dattn_lq2")[:L], f("dattn_lk2")[:L]], axis=1)[None])
    sh["dn"] = np.ascontiguousarray(f("dattn_norm")[:L].T)
    sh.update(_consts(L))
    return sh


_NC_CACHE = {}
_SIM_HOOK = None
_LIMIT = None
_DBG = ""


class StopBuild(Exception):
    pass


def run(inp, TP, L):
    f = lambda k: np.asarray(inp[k], dtype=np.float32)
    key = (TP, L)
    if key not in _NC_CACHE:
        _NC_CACHE[key] = build(TP, L)
    nc = _NC_CACHE[key]
    sh = _shared_inputs(inp, L)
    xp, xs, mem = f("x_prompt"), f("x_sample"), f("mem_prompt")
    ck, cvv = f("cache_attn_k")[:L], f("cache_attn_v")[:L]
    sre, sim, sconv = f("state_ssm_re")[:L], f("state_ssm_im")[:L], f("state_conv")[:L]
    cmk, cmvv = f("cache_mem_k")[:L], f("cache_mem_v")[:L]
    in_maps = []
    for c in range(8):
        b = c // 2
        ss = slice(NSS * c, NSS * c + NSS)
        m = dict(sh)
        m["xT_p"] = np.ascontiguousarray(xp[b].T)
        m["xT_s"] = np.ascontiguousarray(xs[ss].reshape(NSS * TS, D).T)
        m["memT"] = np.ascontiguousarray(mem[b].T)
        m["ckT"] = np.ascontiguousarray(ck[:, ss].reshape(L, NSS, PAST, 256).transpose(0, 1, 3, 2))
        m["cv"] = np.ascontiguousarray(cvv[:, ss].reshape(L, NSS, PAST, 256))
        s0 = np.zeros((128, L, NSS, 16), np.float32)
        s0[0:64] = sre[:, ss].transpose(3, 0, 1, 2)
        s0[64:128] = sim[:, ss].transpose(3, 0, 1, 2)
        m["s0"] = s0
        cb = sconv[:, ss]
        m["cbuf"] = np.ascontiguousarray(cb.reshape(L, NSS, 30, 2, 128).transpose(4, 0, 3, 1, 2))
        m["cmkT"] = np.ascontiguousarray(cmk[:, ss].reshape(L, NSS, 256, 512).transpose(0, 1, 3, 2))
        m["cmv"] = np.ascontiguousarray(cmvv[:, ss].reshape(L, NSS, 256, 512))
        in_maps.append(m)
    if _SIM_HOOK is not None:
        R = _SIM_HOOK(nc, in_maps)
    else:
        res = run_bass_kernel_spmd(nc, in_maps, core_ids=list(range(8)))
        R = res.results
    B = 4
    y_p = np.stack([R[2 * b]["yT_p"].T for b in range(B)])
    y_s = np.concatenate([R[c]["yT_s"].T.reshape(NSS, TS, D) for c in range(8)])
    p_k = np.stack([R[2 * b]["o_kT_p"].transpose(0, 2, 1).reshape(L, TP, 4, 64) for b in range(B)], axis=1)
    p_v = np.stack([R[2 * b]["o_v_p"].reshape(L, TP, 4, 64) for b in range(B)], axis=1)
    p_sre = np.stack([R[2 * b]["o_ssm_p"][:, 0:64, :].transpose(0, 2, 1) for b in range(B)], axis=1)
    p_sim = np.stack([R[2 * b]["o_ssm_p"][:, 64:128, :].transpose(0, 2, 1) for b in range(B)], axis=1)
    p_conv = np.stack([R[2 * b]["o_conv_p"].transpose(0, 2, 1) for b in range(B)], axis=1)
    p_mk = np.stack([R[2 * b]["o_mkT"].transpose(0, 2, 1).reshape(L, 256, 4, 128) for b in range(B)], axis=1)
    p_mv = np.stack([R[2 * b]["o_mv"].reshape(L, 256, 4, 128) for b in range(B)], axis=1)
    s_k = np.concatenate([R[c]["o_kT_s"].transpose(0, 2, 1).reshape(L, NSS, TS, 4, 64) for c in range(8)], axis=1)
    s_v = np.concatenate([R[c]["o_v_s"].reshape(L, NSS, TS, 4, 64) for c in range(8)], axis=1)
    s_sre = np.concatenate([R[c]["o_ssm_s"][:, :, 0:64, :].transpose(0, 1, 3, 2) for c in range(8)], axis=1)
    s_sim = np.concatenate([R[c]["o_ssm_s"][:, :, 64:128, :].transpose(0, 1, 3, 2) for c in range(8)], axis=1)
    s_conv = np.concatenate([R[c]["o_conv_s"].transpose(0, 1, 3, 2) for c in range(8)], axis=1)
    s_gv = np.concatenate([R[c]["o_gv_s"].reshape(L, NSS, TS, 256) for c in range(8)], axis=1)
    outs = (y_p, y_s, p_k, p_v, p_sre, p_sim, p_conv, p_mk, p_mv, s_k, s_v, s_sre, s_sim, s_conv, s_gv)
    return tuple(np.ascontiguousarray(o, dtype=np.float32) for o in outs)


def kernel(**inputs):
    return run(inputs, TP_FULL, L_FULL)
```

```python
import math
import contextlib
import numpy as np
import concourse.bass as bass
import concourse.mybir as mybir
from concourse.bass_utils import run_bass_kernel_spmd

F32 = mybir.dt.float32
BF16 = mybir.dt.bfloat16
ALU = mybir.AluOpType
AF = mybir.ActivationFunctionType

COMPUTE = ("pe", "act", "dve", "pool")
NDMA_SEMS = 24
EPS = 1e-6
L_FULL = 4
TP_FULL = 4096
PAST = 2048
NSS = 4
TS = 64
D = 1024
DFF = 4096


class Buf:
    __slots__ = ("name", "w", "r", "psum")

    def __init__(self, name="", psum=False):
        self.name = name
        self.w = None
        self.r = []
        self.psum = psum


class T:
    __slots__ = ("ap", "buf")

    def __init__(self, ap, buf):
        self.ap = ap
        self.buf = buf

    def __getitem__(self, k):
        return T(self.ap[k], self.buf)

    def re(self, pat, **kw):
        return T(self.ap.rearrange(pat, **kw), self.buf)

    def sub(self, buf):
        return T(self.ap, buf)


class Sched:
    def __init__(self, nc):
        self.nc = nc
        self.streams = {e: [] for e in ("pe", "act", "dve", "pool", "sp")}
        self.count = {e: 0 for e in COMPUTE}
        self.waited = {e: {} for e in self.streams}
        self.dma_cnt = {}
        self.dma_rr = {"sp": 0, "pool": 0, "act": 0}

    def _need(self, eng, tok, waits):
        if tok is None:
            return
        key, val, teng = tok
        if teng == eng and eng == "pe":
            return
        if self.waited[eng].get(key, 0) >= val:
            return
        if waits.get(key, 0) < val:
            waits[key] = val

    def _deps(self, eng, reads, writes):
        waits = {}
        for b in reads:
            self._need(eng, b.w, waits)
            if b.psum:
                for t in b.r:
                    if t[2] != eng:
                        self._need(eng, t, waits)
        for b in writes:
            self._need(eng, b.w, waits)
            for t in b.r:
                self._need(eng, t, waits)
        for k, v in waits.items():
            self.waited[eng][k] = v
        return list(waits.items())

    def _mark(self, tok, reads, writes):
        for b in reads:
            b.r.append(tok)
            if len(b.r) > 16:
                best = {}
                for t in b.r:
                    if t[0] not in best or best[t[0]][1] < t[1]:
                        best[t[0]] = t
                b.r = list(best.values())
        for b in writes:
            b.w = tok
            b.r = []

    def op(self, eng, fn, reads=(), writes=()):
        waits = self._deps(eng, reads, writes)
        self.count[eng] += 1
        tok = (eng, self.count[eng], eng)
        self.streams[eng].append((waits, fn, (eng, 1)))
        self._mark(tok, reads, writes)

    def dma(self, q, fn, reads=(), writes=(), grp=None, ngrp=6):
        if grp is None:
            j = self.dma_rr[q]
            self.dma_rr[q] = (j + 1) % NDMA_SEMS
            key = "dma_%s_%d" % (q, j)
        else:
            j = self.dma_rr.get(grp, 0)
            self.dma_rr[grp] = (j + 1) % ngrp
            key = "dma_%s_%d" % (grp, j)
        prev = self.dma_cnt.get(key, 0)
        waits = dict(self._deps(q, reads, writes))
        if prev > 0 and self.waited[q].get(key, 0) < 16 * prev:
            waits[key] = max(waits.get(key, 0), 16 * prev)
            self.waited[q][key] = 16 * prev
        self.dma_cnt[key] = prev + 1
        tok = (key, 16 * (prev + 1), "dma")
        self.streams[q].append((list(waits.items()), fn, (key, 16)))
        self._mark(tok, reads, writes)
        return tok

    def barrier(self, with_dma=True):
        for e in COMPUTE + ("sp",):
            waits = {}
            for o in COMPUTE:
                if o != e and self.count[o] > self.waited[e].get(o, 0):
                    waits[o] = self.count[o]
            if with_dma:
                for key, cnt in self.dma_cnt.items():
                    if self.waited[e].get(key, 0) < 16 * cnt:
                        waits[key] = 16 * cnt
            for k, v in waits.items():
                self.waited[e][k] = v
            if waits:
                self.streams[e].append((list(waits.items()), None, None))

    def final_wait_all(self, eng="sp"):
        waits = []
        for key, cnt in self.dma_cnt.items():
            waits.append((key, 16 * cnt))
        for e in COMPUTE:
            if self.count[e] > 0:
                waits.append((e, self.count[e]))
        self.streams[eng].append((waits, None, None))

    def emit(self):
        nc = self.nc
        keys = set(COMPUTE)
        for k in self.dma_cnt:
            keys.add(k)
        keys = sorted(keys)
        with contextlib.ExitStack() as st:
            sems = {k: st.enter_context(nc.semaphore("s_" + k)) for k in keys}
            block = st.enter_context(nc.Block())
            engs = {"pe": block.tensor, "act": block.scalar, "dve": block.vector,
                    "pool": block.gpsimd, "sp": block.sync}

            def mk(name):
                items = self.streams[name]

                def body(e):
                    for waits, fn, inc in items:
                        for k, v in waits:
                            e.wait_ge(sems[k], v)
                        if fn is not None:
                            fn(e).then_inc(sems[inc[0]], inc[1])
                return body
            for name in ("pe", "act", "dve", "pool", "sp"):
                engs[name](mk(name))


class Prog:
    def __init__(self, TP, L):
        self.TP, self.L = TP, L
        self.nc = bass.Bass("TRN2", target_bir_lowering=False)
        self.S = Sched(self.nc)
        self.st = contextlib.ExitStack()
        self.dr = {}

    def din(self, name, shape):
        t = self.nc.dram_tensor(name, list(shape), F32, kind="ExternalInput").ap()
        self.dr[name] = T(t, Buf(name))
        return self.dr[name]

    def dout(self, name, shape):
        t = self.nc.dram_tensor(name, list(shape), F32, kind="ExternalOutput").ap()
        self.dr[name] = T(t, Buf(name))
        return self.dr[name]

    def dscratch(self, name, shape, dt=F32):
        t = self.nc.dram_tensor(name, list(shape), dt, kind="Internal").ap()
        self.dr[name] = T(t, Buf(name))
        return self.dr[name]

    def sb(self, name, shape, dt=F32):
        t = self.st.enter_context(self.nc.sbuf_tensor(name, list(shape), dt))
        return T(t[:], Buf(name))

    def MM(self, o, l, r, start=True, stop=True, **kw):
        self.S.op("pe", lambda e: e.matmul(o.ap, lhsT=l.ap, rhs=r.ap, start=start, stop=stop, **kw),
                  reads=[l.buf, r.buf], writes=[o.buf])

    def ACT(self, o, i, func, bias=None, scale=None, eng="act"):
        reads = [i.buf]
        kw = {}
        if bias is not None:
            if isinstance(bias, T):
                reads.append(bias.buf)
                kw["bias"] = bias.ap
            else:
                kw["bias"] = float(bias)
        if scale is not None:
            if isinstance(scale, T):
                reads.append(scale.buf)
                kw["scale"] = scale.ap
            else:
                kw["scale"] = float(scale)
        self.S.op("act", lambda e: e.activation(out=o.ap, in_=i.ap, func=func, **kw), reads=reads, writes=[o.buf])

    def TT(self, o, a, b, op, eng="dve"):
        self.S.op(eng, lambda e: e.tensor_tensor(out=o.ap, in0=a.ap, in1=b.ap, op=op), reads=[a.buf, b.buf], writes=[o.buf])

    def TS(self, o, a, s1, s2=None, op0=ALU.mult, op1=None, eng="dve"):
        reads = [a.buf]
        v1 = s1
        v2 = s2
        if isinstance(s1, T):
            reads.append(s1.buf)
            v1 = s1.ap
        if isinstance(s2, T):
            reads.append(s2.buf)
            v2 = s2.ap
        if op1 is None:
            self.S.op(eng, lambda e: e.tensor_scalar(out=o.ap, in0=a.ap, scalar1=v1, scalar2=None, op0=op0), reads=reads, writes=[o.buf])
        else:
            self.S.op(eng, lambda e: e.tensor_scalar(out=o.ap, in0=a.ap, scalar1=v1, scalar2=v2, op0=op0, op1=op1), reads=reads, writes=[o.buf])

    def STT(self, o, a, s, b, op0, op1, eng="dve"):
        reads = [a.buf, b.buf]
        v = s
        if isinstance(s, T):
            reads.append(s.buf)
            v = s.ap
        self.S.op(eng, lambda e: e.scalar_tensor_tensor(out=o.ap, in0=a.ap, scalar=v, in1=b.ap, op0=op0, op1=op1), reads=reads, writes=[o.buf])

    def CP(self, o, i, eng="dve", real=False):
        if real or eng != "dve":
            self.S.op(eng, lambda e: e.tensor_copy(out=o.ap, in_=i.ap), reads=[i.buf], writes=[o.buf])
        else:
            self.S.op(eng, lambda e: e.tensor_scalar(out=o.ap, in0=i.ap, scalar1=1.0, scalar2=None, op0=ALU.mult), reads=[i.buf], writes=[o.buf])

    def RECIP(self, o, i):
        self.S.op("dve", lambda e: e.reciprocal(out=o.ap, in_=i.ap), reads=[i.buf], writes=[o.buf])

    def MEMSET(self, o, v, eng="dve"):
        self.S.op(eng, lambda e: e.memset(o.ap, v), writes=[o.buf])

    def SCAN(self, o, d0, d1, init):
        reads = [d0.buf, d1.buf]
        iv = init
        if isinstance(init, T):
            reads.append(init.buf)
            iv = init.ap
        self.S.op("dve", lambda e: e.tensor_tensor_scan(out=o.ap, data0=d0.ap, data1=d1.ap, initial=iv, op0=ALU.mult, op1=ALU.add),
                  reads=reads, writes=[o.buf])

    def DMA(self, q, o, i, grp=None):
        if len(o.ap.shape) > 3 and len(i.ap.shape) == len(o.ap.shape):
            names = " ".join("d%d" % k for k in range(1, len(o.ap.shape)))
            pat = "p %s -> p (%s)" % (names, names)
            try:
                o2, i2 = o.re(pat), i.re(pat)
                o, i = o2, i2
            except Exception:
                pass
        self.S.dma(q, lambda e: e.dma_start(out=o.ap, in_=i.ap), reads=[i.buf], writes=[o.buf], grp=grp)


def build(TP=TP_FULL, L=L_FULL):
    P = Prog(TP, L)
    phc = [0]

    def phase_done():
        phc[0] += 1
        if _LIMIT is not None and phc[0] >= _LIMIT:
            raise StopBuild()
    nc, S = P.nc, P.S
    NT = TP // 512
    NKT = TP // 128
    NS = NSS * TS

    xT_p = P.din("xT_p", [D, TP])
    xT_s = P.din("xT_s", [D, NS])
    memT = P.din("memT", [D, 256])
    ckT = P.din("ckT", [L, NSS, 256, PAST])
    cv = P.din("cv", [L, NSS, PAST, 256])
    s0 = P.din("s0", [128, L, NSS, 16])
    cbuf = P.din("cbuf", [128, L, 2, NSS, 30])
    cmkT = P.din("cmkT", [L, NSS, 512, 256])
    cmv = P.din("cmv", [L, NSS, 256, 512])
    Wn = {}
    for nm, shp in [("ffn1_w_gate", [L, D, DFF]), ("ffn1_w_up", [L, D, DFF]), ("ffn1_w_down", [L, DFF, D]),
                    ("ffn2_w_gate", [L, D, DFF]), ("ffn2_w_up", [L, D, DFF]), ("ffn2_w_down", [L, DFF, D]),
                    ("w_in", [L, D, 2048]), ("w_out", [L, D, D]), ("xattn_wq", [L, D, 512]), ("xattn_wk", [L, D, 512]),
                    ("xattn_wv", [L, D, 512]), ("xattn_wo", [L, 512, D]), ("ssm_w_glu", [L, 256, 256]),
                    ("conv_w_pw", [L, 256, 256])]:
        Wn[nm] = P.din(nm, shp)
    gains = P.din("gains", [128, L, 5, 8])
    fin_g = P.din("fin_g", [128, 8])
    Bx_d = P.din("Bx", [L, 128, 16, 128])
    Bxs_d = P.din("Bxs", [L, 128, 16, 128])
    Cx1_d = P.din("Cx1", [L, 128, 16, 128])
    Cx2_d = P.din("Cx2", [L, 128, 16, 128])
    ssm_sc = P.din("ssm_sc", [128, L, 3, 16])
    colp = P.din("colp", [128, L, 7, 2])
    convw = P.din("convw", [128, L, 2, 31])
    gm_gb = P.din("gm_gb", [128, L, 2, 256])
    wsT = P.din("wsT", [L, 128, 4, 128])
    bs_bc = P.din("bs_bc", [128, L, 2, 128])
    bs_bc_s = P.din("bs_bc_s", [128, L, 2, 128])
    lqk = P.din("lqk", [1, L, 4, 32])
    dn = P.din("dn", [64, L])
    c_ident = P.din("c_ident", [128, 128])
    c_alibi = P.din("c_alibi", [128, 4, 33])
    c_diag = P.din("c_diag", [128, 4, 128])
    c_tril = P.din("c_tril", [128, 128])
    c_sel = P.din("c_sel", [65, 64])
    c_misc = P.din("c_misc", [128, 66])
    c_lam = P.din("c_lam", [64, L, 2])

    yT_p = P.dout("yT_p", [D, TP])
    yT_s = P.dout("yT_s", [D, NS])
    o_kT_p = P.dout("o_kT_p", [L, 256, TP])
    o_v_p = P.dout("o_v_p", [L, TP, 256])
    o_ssm_p = P.dout("o_ssm_p", [L, 128, 16])
    o_conv_p = P.dout("o_conv_p", [L, 256, 30])
    o_mkT = P.dout("o_mkT", [L, 512, 256])
    o_mv = P.dout("o_mv", [L, 256, 512])
    o_kT_s = P.dout("o_kT_s", [L, 256, NS])
    o_v_s = P.dout("o_v_s", [L, NS, 256])
    o_ssm_s = P.dout("o_ssm_s", [L, NSS, 128, 16])
    o_conv_s = P.dout("o_conv_s", [L, NSS, 256, 30])
    o_gv_s = P.dout("o_gv_s", [L, NS, 256])
    s5tab = P.dscratch("s5tab", [L, 128, 5 * 1024])
    kth_d = P.dscratch("kth_d", [L, 128, 2, TP], BF16)
    vah_d = P.dscratch("vah_d", [L, 128, NKT, 260], BF16)

    x = P.sb("x", [128, 8, 512])
    h = P.sb("h", [128, 8, 512], BF16)
    NRING = 4
    ring = [P.sb("ring%d" % i, [128, 4096], BF16) for i in range(NRING)]
    ident_f = P.sb("ident_f", [128, 128])
    ident_b = P.sb("ident_b", [128, 128], BF16)
    ones_b = P.sb("ones_b", [128, 128], BF16)
    onesf64 = P.sb("onesf64", [64, 64])
    onesf128 = P.sb("onesf128", [128, 128])
    alibi = P.sb("alibi", [128, 4, 33])
    diag = P.sb("diag", [128, 4, 128])
    tril_b = P.sb("tril_b", [128, 128], BF16)
    sel = P.sb("sel", [65, 64])
    misc = P.sb("misc", [128, 66])
    gains_sb = P.sb("gains_sb", [128, L, 5, 8])
    fin_sb = P.sb("fin_sb", [128, 8])
    colp_sb = P.sb("colp_sb", [128, L, 7, 2])
    convw_sb = P.sb("convw_sb", [128, L, 2, 31])
    bs_sb = P.sb("bs_sb", [128, L, 2, 128])
    bs_sb_s = P.sb("bs_sb_s", [128, L, 2, 128])
    dn_sb = P.sb("dn_sb", [64, L])
    clam_sb = P.sb("clam_sb", [64, L, 2])
    lamneg = P.sb("lamneg", [64, L])
    gcol = P.sb("gcol", [64, L])
    Etab = P.sb("Etab", [128, L, 2, 16])
    sst = [P.sb("sst%d" % l, [128, 16]) for l in range(L)]
    ctail = [P.sb("ctail%d" % l, [128, 2, 30], BF16) for l in range(L)]
    KTh = P.sb("KTh", [128, 2, TP], BF16)
    VAh = P.sb("VAh", [128, NKT, 4, 65], BF16)
    KTown_p = P.sb("KTown", [128, 2, NS], BF16)
    VAown_p = P.sb("VAown", [64, NSS, 4, 65], BF16)
    u_b = P.sb("u_b", [128, 2, 512], BF16)
    u_f = P.sb("u_f", [128, 2, 512])
    QT = P.sb("QT", [128, 2, 512], BF16)
    zp = P.sb("zp", [128, 2, NSS * 94 + 200], BF16)
    gu = P.sb("gu", [128, 2, 512])
    mix = P.sb("mix", [128, 8, 512], BF16)
    ARENA_F32 = 16 * 1024
    arena = P.sb("arena", [128, ARENA_F32])
    PS = P.st.enter_context(nc.psum_tensor("PS", [128, 8, 512], F32))
    psb = [Buf("ps%d" % i, psum=True) for i in range(8)]

    def ps(b, parts=128, cols=512, c0=0):
        return T(PS[0:parts, b, c0:c0 + cols], psb[b])

    class Arena:
        def __init__(self):
            self.off = 0

        def reset(self):
            S.barrier()
            self.off = 0

        def alloc(self, n_elems, dt=F32, parts=128, name=""):
            nf = n_elems if dt == F32 else (n_elems + 1) // 2
            assert self.off + nf <= ARENA_F32, (name, self.off, nf)
            ap = arena.ap[0:parts, self.off:self.off + nf]
            self.off += nf
            if dt != F32:
                ap = ap.bitcast(dt)
            return T(ap, Buf(name))
    A = Arena()

    WB = {}
    for nm_ in ["ffn1_w_gate", "ffn1_w_up", "ffn1_w_down", "ffn2_w_gate", "ffn2_w_up", "ffn2_w_down", "w_in", "w_out",
                "xattn_wq", "xattn_wk", "xattn_wv", "xattn_wo"]:
        shp_ = list(Wn[nm_].ap.shape)
        tb_ = nc.dram_tensor(nm_ + "_b16", shp_, BF16, kind="Internal").ap()
        WB[nm_] = [T(tb_[l_], Buf("%s_b16_%d" % (nm_, l_))) for l_ in range(L)]

    def convert_weights(l_, only=None):
        order = ["ffn1_w_gate", "ffn1_w_up", "ffn1_w_down", "w_in", "w_out", "xattn_wq", "xattn_wo",
                 "ffn2_w_gate", "ffn2_w_up", "ffn2_w_down"]
        if only is not None:
            order = only
        if True:
            for nm_ in order:
                src = T(Wn[nm_].ap[l_], Wn[nm_].buf)
                dst = WB[nm_][l_]
                rows, cols = src.ap.shape
                if cols > 2048:
                    src = src.re("r (a b) -> r a b", b=2048)
                    dst = dst.re("r (a b) -> r a b", b=2048)
                half = rows // 2
                P.DMA("pool", dst[0:half], src[0:half], grp="cv")
                P.DMA("pool", dst[half:rows], src[half:rows], grp="cv")

    wq_list = []
    wstate = {"issued": 0, "taken": 0}

    def w_issue_upto(n):
        while wstate["issued"] < min(n, len(wq_list)):
            i = wstate["issued"]
            tag, src, shp = wq_list[i]
            slot = ring[i % NRING]
            ne = 1
            for s_ in shp:
                ne *= s_
            dst = slot[:, 0:ne]
            if len(shp) == 2:
                dst = dst.re("p (a b) -> p a b", a=shp[0])
            P.DMA("sp", dst, src)
            wstate["issued"] += 1

    def w_next(tag):
        i = wstate["taken"]
        assert wq_list[i][0] == tag, (wq_list[i][0], tag)
        w_issue_upto(i + NRING - 1)
        wstate["taken"] += 1
        shp = wq_list[i][2]
        ne = 1
        for s_ in shp:
            ne *= s_
        v = ring[i % NRING][:, 0:ne]
        if len(shp) == 2:
            v = v.re("p (a b) -> p a b", a=shp[0])
        return v

    def kview(w, l, r0, nr, c0, ncol):
        return T(w.ap[l, r0:r0 + nr, c0:c0 + ncol].rearrange("(k p) n -> p k n", p=128), w.buf)

    def bview(nm, l, r0, nr, c0, ncol):
        wb = WB[nm][l]
        return T(wb.ap[r0:r0 + nr, c0:c0 + ncol].rearrange("(k p) n -> p k n", p=128), wb.buf)

    def plan_ffn(l, which):
        g, u, d = which + "_w_gate", which + "_w_up", which + "_w_down"
        for half in range(2):
            for jc in range(4):
                c0 = 2048 * half + 512 * jc
                wq_list.append((which + "g", bview(g, l, 0, D, c0, 512), (8, 512)))
                wq_list.append((which + "u", bview(u, l, 0, D, c0, 512), (8, 512)))
            for oc in range(4):
                wq_list.append((which + "d", bview(d, l, 2048 * half, 2048, 256 * oc, 256), (16, 256)))

    def plan_layer(l):
        plan_ffn(l, "ffn1")
        for c in range(4):
            wq_list.append(("win", bview("w_in", l, 0, D, 512 * c, 512), (8, 512)))
        for c in range(2):
            wq_list.append(("wout", bview("w_out", l, 0, D, 512 * c, 512), (8, 512)))
        wq_list.append(("wq", bview("xattn_wq", l, 0, D, 0, 512), (8, 512)))
        wq_list.append(("wo", bview("xattn_wo", l, 0, 512, 0, D), (4, 1024)))
        plan_ffn(l, "ffn2")

    for l in range(L):
        wq_list.append(("wk", bview("xattn_wk", l, 0, D, 0, 512), (8, 512)))
        wq_list.append(("wv", bview("xattn_wv", l, 0, D, 0, 512), (8, 512)))
    tiles = [("p", i) for i in range(NT)] + [("s", 0)]
    for _t in tiles:
        for l in range(L):
            plan_layer(l)

    for dst, src in [(ident_f, c_ident), (alibi, c_alibi), (diag, c_diag), (sel, c_sel), (misc, c_misc),
                     (gains_sb, gains), (fin_sb, fin_g), (colp_sb, colp), (convw_sb, convw), (bs_sb, bs_bc), (bs_sb_s, bs_bc_s),
                     (dn_sb, dn), (clam_sb, c_lam)]:
        P.DMA("pool", dst, src)
    P.DMA("pool", ident_b, c_ident)
    P.DMA("pool", tril_b, c_tril)
    P.MEMSET(ones_b, 1.0)
    P.MEMSET(onesf64, 1.0 / 64.0)
    P.MEMSET(onesf128, 1.0 / 256.0)
    P.MEMSET(VAh[:, :, :, 64:65], 1.0)
    P.MEMSET(VAown_p[:, :, :, 64:65], 1.0)
    for l in range(L):
        P.MEMSET(sst[l], 0.0)
        P.MEMSET(ctail[l], 0.0)

    def rmsnorm_to_h(N, gcolT):
        sq = A.alloc(8 * 512, BF16, name="sq").re("p (k n) -> p k n", k=8)
        for k in range(8):
            P.ACT(sq[:, k, 0:N], x[:, k, 0:N], AF.Square)
        acc = ps(6, cols=N)
        for k in range(8):
            P.MM(acc, ones_b, sq[:, k, 0:N], start=(k == 0), stop=(k == 7))
        rs = A.alloc(512, name="rs")
        P.ACT(rs[:, 0:N], acc, AF.Sqrt, bias=EPS, scale=1.0 / D)
        P.RECIP(rs[:, 0:N], rs[:, 0:N])
        for k in range(8):
            P.STT(h[:, k, 0:N], x[:, k, 0:N], gcolT[:, k:k + 1], rs[:, 0:N], ALU.mult, ALU.mult)

    def ffn(l, which, N, gidx):
        A.reset()
        rmsnorm_to_h(N, gains_sb[:, l, gidx, :])
        hid = A.alloc(16 * 512, BF16, name="hid").re("p (k n) -> p k n", k=16)
        sg = [A.alloc(512, name="sg%d" % i) for i in range(2)]
        cnt = 0
        for half in range(2):
            for jc in range(4):
                wg = w_next(which + "g")
                wu = w_next(which + "u")
                for m in range(4):
                    G = ps(cnt % 2, cols=N)
                    U = ps(2 + cnt % 2, cols=N)
                    for k in range(8):
                        P.MM(G, wg[:, k, 128 * m:128 * m + 128], h[:, k, 0:N], start=(k == 0), stop=(k == 7))
                    for k in range(8):
                        P.MM(U, wu[:, k, 128 * m:128 * m + 128], h[:, k, 0:N], start=(k == 0), stop=(k == 7))
                    s_ = sg[cnt % 2]
                    P.ACT(s_[:, 0:N], G, AF.Silu)
                    P.TT(hid[:, 4 * jc + m, 0:N], s_[:, 0:N], U, ALU.mult)
                    cnt += 1
            for oc in range(4):
                wd = w_next(which + "d")
                for mi in range(2):
                    i = 2 * oc + mi
                    O = ps(4 + i % 2, cols=N)
                    for k in range(16):
                        P.MM(O, wd[:, k, 128 * mi:128 * mi + 128], hid[:, k, 0:N], start=(k == 0), stop=(k == 15))
                    P.STT(x[:, i, 0:N], O, 0.5, x[:, i, 0:N], ALU.mult, ALU.add)

    def prep():
        A.reset()
        sc = A.alloc(L * 3 * 16, name="sc").re("p (l a g) -> p l a g", l=L, a=3)
        P.DMA("pool", sc, ssm_sc)
        lq = A.alloc(L * 4 * 32, parts=1, name="lq").re("p (l a d) -> p l a d", l=L, a=4)
        P.DMA("pool", lq, lqk)
        pr = A.alloc(L * 2 * 32, parts=1, name="pr").re("p (l a d) -> p l a d", l=L, a=2)
        dots = A.alloc(L * 2, parts=1, name="dots")
        for l in range(L):
            for a in range(2):
                P.TT(pr[:, l, a, :], lq[:, l, 2 * a, :], lq[:, l, 2 * a + 1, :], ALU.mult)
                S.op("dve", (lambda l=l, a=a: (lambda e: e.reduce_sum(out=dots.ap[:, 2 * l + a:2 * l + a + 1], in_=pr.ap[:, l, a, :], axis=mybir.AxisListType.X)))(),
                     reads=[pr.buf], writes=[dots.buf])
        ex = A.alloc(L * 2, parts=1, name="ex")
        P.ACT(ex, dots, AF.Exp)
        lam1 = A.alloc(L, parts=1, name="lam1")
        exv = ex.re("p (l a) -> p l a", a=2)
        P.TT(lam1, exv[:, :, 0], exv[:, :, 1], ALU.subtract)
        ones1 = A.alloc(64, parts=1, name="ones1")
        P.MEMSET(ones1, 1.0)
        pl = ps(7, parts=64, cols=L)
        P.MM(pl, ones1, lam1)
        P.TT(lamneg, pl, clam_sb[:, :, 0], ALU.add)
        P.TS(lamneg, lamneg, -1.0)
        P.TT(gcol, dn_sb, clam_sb[:, :, 1], ALU.mult)
        for l in range(L):
            dt = A.alloc(16, name="dt")
            P.ACT(dt, sc[:, l, 2, :], AF.Exp)
            ard = A.alloc(16, name="ard")
            th = A.alloc(16, name="th")
            P.TT(ard, sc[:, l, 0, :], dt, ALU.mult)
            P.TT(th, sc[:, l, 1, :], dt, ALU.mult)
            r = A.alloc(16, name="r")
            P.ACT(r, ard, AF.Exp)
            tmp = A.alloc(16, name="tmp")
            sn = A.alloc(16, name="sn")
            cs = A.alloc(16, name="cs")
            tmi = A.alloc(16, name="tmi")
            tmk = A.alloc(16, name="tmk")
            sincos(sn, th, tmp, tmi, tmk, 8.0)
            sincos(cs, th, tmp, tmi, tmk, 8.25)
            nr = A.alloc(16, name="nr")
            ni = A.alloc(16, name="ni")
            P.TT(nr, r, cs, ALU.mult)
            P.TS(nr, nr, -1.0, None, ALU.add)
            P.TT(ni, r, sn, ALU.mult)
            den = A.alloc(16, name="den")
            t2 = A.alloc(16, name="t2")
            P.TT(den, sc[:, l, 0, :], sc[:, l, 0, :], ALU.mult)
            P.TT(t2, sc[:, l, 1, :], sc[:, l, 1, :], ALU.mult)
            P.TT(den, den, t2, ALU.add)
            P.RECIP(den, den)
            kr = A.alloc(16, name="kr")
            ki = A.alloc(16, name="ki")
            P.TT(kr, nr, sc[:, l, 0, :], ALU.mult)
            P.TT(t2, ni, sc[:, l, 1, :], ALU.mult)
            P.TT(kr, kr, t2, ALU.add)
            P.TT(kr, kr, den, ALU.mult)
            P.TT(ki, ni, sc[:, l, 0, :], ALU.mult)
            P.TT(t2, nr, sc[:, l, 1, :], ALU.mult)
            P.TT(ki, ki, t2, ALU.subtract)
            P.TT(ki, ki, den, ALU.mult)
            PH = A.alloc(1024, name="PH").re("p (g j) -> p g j", g=16)
            SN = A.alloc(1024, name="SN").re("p (g j) -> p g j", g=16)
            CS = A.alloc(1024, name="CS").re("p (g j) -> p g j", g=16)
            TMP = A.alloc(1024, name="TMP").re("p (g j) -> p g j", g=16)
            TB = A.alloc(5 * 1024, name="TB").re("p (a g j) -> p a g j", a=5, g=16)
            for g in range(16):
                P.TS(PH[:, g, :], misc[:, 0:64], th[:, g:g + 1])
            TMI = A.alloc(1024, name="TMI").re("p (g j) -> p g j", g=16)
            TMK = A.alloc(1024, name="TMK").re("p (g j) -> p g j", g=16)
            sincos(SN, PH, TMP, TMI, TMK, 8.0)
            sincos(CS, PH, TMP, TMI, TMK, 8.25)
            sgn = misc[:, 64:65]
            for g in range(16):
                P.TS(TB[:, 0, g, :], CS[:, g, :], kr[:, g:g + 1])
                P.STT(TB[:, 0, g, :], SN[:, g, :], ki[:, g:g + 1], TB[:, 0, g, :], ALU.mult, ALU.add)
                P.TS(TB[:, 1, g, :], CS[:, g, :], ki[:, g:g + 1])
                P.TS(TMP[:, g, :], SN[:, g, :], kr[:, g:g + 1])
                P.TS(TB[:, 4, g, :], misc[:, 0:64], 0.0, r[:, g:g + 1], ALU.mult, ALU.add)
            P.TT(TB[:, 1], TB[:, 1], TMP, ALU.subtract)
            P.TS(TB[:, 1], TB[:, 1], sgn)
            P.TS(TB[:, 2], CS, sgn, -1.0, ALU.mult, ALU.mult)
            P.TS(TB[:, 3], SN, -1.0)
            P.CP(Etab[:, l, 0, :], CS[:, :, 63])
            P.TS(Etab[:, l, 1, :], SN[:, :, 63], sgn)
            P.DMA("pool", T(s5tab.ap[l], s5tab.buf), TB.re("p a g j -> p (a g j)"))
            A.off -= (16 * 15 + 11 * 1024)
            S.barrier()
        A.reset()
        memf = A.alloc(8 * 256, name="memf").re("p (k n) -> p k n", k=8)
        P.DMA("pool", memf, T(memT.ap.rearrange("(k p) n -> p k n", p=128), memT.buf))
        sq = A.alloc(8 * 256, BF16, name="msq").re("p (k n) -> p k n", k=8)
        for k in range(8):
            P.ACT(sq[:, k, :], memf[:, k, :], AF.Square)
        acc = ps(6, cols=256)
        for k in range(8):
            P.MM(acc, ones_b, sq[:, k, :], start=(k == 0), stop=(k == 7))
        rs = A.alloc(256, name="mrs")
        P.ACT(rs, acc, AF.Sqrt, bias=EPS, scale=1.0 / D)
        P.RECIP(rs, rs)
        mh = A.alloc(8 * 256, BF16, name="mh").re("p (k n) -> p k n", k=8)
        stg = A.alloc(1024, name="stg")
        for l in range(L):
            for k in range(8):
                P.STT(mh[:, k, :], memf[:, k, :], gains_sb[:, l, 3, k:k + 1], rs, ALU.mult, ALU.mult)
            wk = w_next("wk")
            wv = w_next("wv")
            for m in range(4):
                o = ps(m % 2, cols=256)
                for k in range(8):
                    P.MM(o, wk[:, k, 128 * m:128 * m + 128], mh[:, k, :], start=(k == 0), stop=(k == 7))
                st_ = stg[:, 256 * (m % 2):256 * (m % 2) + 256]
                P.ACT(st_, o, AF.Copy)
                P.DMA("pool", T(o_mkT.ap[l, 128 * m:128 * m + 128, :], o_mkT.buf), st_)
            for c in range(2):
                o = ps(2 + c, cols=512)
                for k in range(8):
                    P.MM(o, mh[:, k, 128 * c:128 * c + 128], wv[:, k, :], start=(k == 0), stop=(k == 7))
                st_ = stg[:, 512 * c:512 * c + 512]
                P.ACT(st_, o, AF.Copy)
                P.DMA("pool", T(o_mv.ap[l, 128 * c:128 * c + 128, :], o_mv.buf), st_)

    def sincos(dst, phi, tmp, tmi, tmk, shift):
        I32 = mybir.dt.int32
        P.TS(tmp, phi, 1.0 / (2 * math.pi), shift, ALU.mult, ALU.add)
        ti_ = T(tmi.ap.bitcast(I32), tmi.buf)
        P.CP(ti_, tmp, real=True)
        P.CP(tmk, ti_, real=True)
        P.TT(tmp, tmp, tmk, ALU.subtract)
        P.TS(tmk, tmp, 0.5, None, ALU.is_gt)
        P.TT(tmp, tmp, tmk, ALU.subtract)
        P.ACT(dst, tmp, AF.Sin, scale=2 * math.pi)

    def mixer(l, kind, ti):
        N = 512 if kind == "p" else NS
        t0 = 512 * ti
        nsub = N // 64
        A.reset()
        rmsnorm_to_h(N, gains_sb[:, l, 1, :])
        cp = colp_sb[:, l]
        kstage = A.alloc(2 * 512, name="kstage").re("p (k n) -> p k n", k=2)
        vstage = A.alloc(4 * 256, name="vstage").re("p (c n) -> p c n", c=4)
        gvt = A.alloc(4 * 256, name="gvt").re("p (c n) -> p c n", c=4)
        if kind == "s":
            KTown, VAown = KTown_p, VAown_p
        if kind == "p" and ti > 0:
            P.DMA("pool", KTh[:, :, 0:t0], T(kth_d.ap[l, :, :, 0:t0], kth_d.buf))
            P.DMA("pool", VAh[:, 0:4 * ti].re("p j h d -> p j (h d)"), T(vah_d.ap[l, :, 0:4 * ti, :], vah_d.buf))
        zsig = A.alloc(2 * 512, name="zsig").re("p (k n) -> p k n", k=2)
        w = w_next("win")
        for m in range(4):
            o = ps(m % 2, cols=N)
            for k in range(8):
                P.MM(o, w[:, k, 128 * m:128 * m + 128], h[:, k, 0:N], start=(k == 0), stop=(k == 7))
            if m < 2:
                P.ACT(u_f[:, m, 0:N], o, AF.Copy)
                P.CP(u_b[:, m, 0:N], u_f[:, m, 0:N])
            else:
                P.TS(QT[:, m - 2, 0:N], o, 32.0 ** -0.5)
        phase_done()
        w = w_next("win")
        for m in range(2):
            o = ps(m % 2, cols=N)
            for k in range(8):
                P.MM(o, w[:, k, 128 * m:128 * m + 128], h[:, k, 0:N], start=(k == 0), stop=(k == 7))
            P.ACT(kstage[:, m, 0:N], o, AF.Copy)
            if kind == "p":
                P.CP(KTh[:, m, t0:t0 + N], kstage[:, m, 0:N])
                P.DMA("pool", T(o_kT_p.ap[l, 128 * m:128 * m + 128, t0:t0 + N], o_kT_p.buf), kstage[:, m, 0:N])
            else:
                P.CP(KTown[:, m, :], kstage[:, m, 0:N])
                P.DMA("pool", T(o_kT_s.ap[l, 128 * m:128 * m + 128, :], o_kT_s.buf), kstage[:, m, 0:N])
        if kind == "p":
            for c in range(4):
                o = ps(2 + c % 2, cols=256)
                for k in range(8):
                    P.MM(o, h[:, k, 128 * c:128 * c + 128], w[:, k, 256:512], start=(k == 0), stop=(k == 7))
                P.ACT(vstage[:, c, :], o, AF.Copy)
                P.CP(VAh[:, 4 * ti + c, :, 0:64], vstage[:, c, :].re("p (h d) -> p h d", h=4))
            P.DMA("pool", T(o_v_p.ap[l, t0:t0 + 512, :].rearrange("(c p) n -> p c n", p=128), o_v_p.buf), vstage)
            if ti < NT - 1:
                P.DMA("pool", T(kth_d.ap[l, :, :, t0:t0 + 512], kth_d.buf), KTh[:, :, t0:t0 + 512])
                P.DMA("pool", T(vah_d.ap[l, :, 4 * ti:4 * ti + 4, :], vah_d.buf), VAh[:, 4 * ti:4 * ti + 4].re("p j h d -> p j (h d)"))
        else:
            for s_ in range(NSS):
                o = ps(2 + s_ % 2, parts=64, cols=256)
                for k in range(8):
                    P.MM(o, h[:, k, 64 * s_:64 * s_ + 64], w[:, k, 256:512], start=(k == 0), stop=(k == 7))
                P.ACT(vstage[0:64, s_, :], o, AF.Copy)
                P.CP(VAown[:, s_, :, 0:64], vstage[0:64, s_, :].re("p (h d) -> p h d", h=4))
            P.DMA("pool", T(o_v_s.ap[l].rearrange("(s p) n -> p s n", p=64), o_v_s.buf), vstage[0:64])
        phase_done()
        w = w_next("win")
        for m in range(2):
            og = ps(m % 2, cols=N)
            for k in range(8):
                P.MM(og, w[:, k, 256 + 128 * m:256 + 128 * m + 128], h[:, k, 0:N], start=(k == 0), stop=(k == 7))
            oz = ps(2 + m % 2, cols=N)
            for k in range(8):
                P.MM(oz, w[:, k, 128 * m:128 * m + 128], h[:, k, 0:N], start=(k == 0), stop=(k == 7))
            P.ACT(zsig[:, m, 0:N], og, AF.Sigmoid)
            P.TT(zsig[:, m, 0:N], zsig[:, m, 0:N], oz, ALU.mult)
        phase_done()
        w = w_next("win")
        for m in range(2):
            o = ps(m % 2, cols=N)
            for k in range(8):
                P.MM(o, w[:, k, 128 * m:128 * m + 128], h[:, k, 0:N], start=(k == 0), stop=(k == 7))
            P.ACT(gu[:, m, 0:N], o, AF.Gelu_apprx_tanh)
        nch = N // 128
        for c in range(nch):
            o = ps(2 + c % 2, cols=256)
            for k in range(8):
                P.MM(o, h[:, k, 128 * c:128 * c + 128], w[:, k, 256:512], start=(k == 0), stop=(k == 7))
            P.ACT(gvt[:, c, :], o, AF.Gelu_apprx_tanh)

        phase_done()
        st6 = A.alloc(nch * 6, name="st6").re("p (c s) -> p c s", c=nch)
        mv2 = A.alloc(nch * 2, name="mv2").re("p (c s) -> p c s", c=nch)
        rstd = A.alloc(nch, name="grstd")
        vb = A.alloc(nch * 256, BF16, name="vb").re("p (c n) -> p c n", c=nch)
        wsb = A.alloc(512, BF16, name="wsb").re("p (h i) -> p h i", h=4)
        wsr = A.alloc(512, BF16, name="wsr").re("p (h i) -> p h i", h=4)
        if kind == "p":
            P.DMA("pool", wsr, T(wsT.ap[l], wsT.buf))
            bsv = bs_sb
        else:
            P.MEMSET(wsr, 0.0)
            P.DMA("pool", wsr[0:64, :, 0:64], T(wsT.ap[l, 0:64, :, 0:64], wsT.buf))
            P.DMA("pool", wsr[64:128, :, 64:128], T(wsT.ap[l, 0:64, :, 0:64], wsT.buf))
            bsv = bs_sb_s
        for hh in range(4):
            P.TT(wsb[:, hh, :], wsr[:, hh, :], tril_b, ALU.mult)
        for c in range(nch):
            S.op("dve", (lambda c=c: (lambda e: e.bn_stats(out=st6.ap[:, c, :], in_=gvt.ap[:, c, :])))(), reads=[gvt.buf], writes=[st6.buf])
            S.op("dve", (lambda c=c: (lambda e: e.bn_aggr(out=mv2.ap[:, c, :], in_=st6.ap[:, c, :])))(), reads=[st6.buf], writes=[mv2.buf])
        P.ACT(rstd, mv2[:, :, 1], AF.Sqrt, bias=EPS)
        P.RECIP(rstd, rstd)
        for c in range(nch):
            P.TS(gvt[:, c, :], gvt[:, c, :], mv2[:, c, 0:1], rstd[:, c:c + 1], ALU.subtract, ALU.mult)
            P.TT(gvt[:, c, :], gvt[:, c, :], T(gm_sb.ap[:, l, 0, :], gm_sb.buf), ALU.mult)
            P.TT(gvt[:, c, :], gvt[:, c, :], T(gm_sb.ap[:, l, 1, :], gm_sb.buf), ALU.add)
            P.CP(vb[:, c, :], gvt[:, c, :])
        if kind == "s":
            P.DMA("pool", T(o_gv_s.ap[l].rearrange("(c p) n -> p c n", p=128), o_gv_s.buf), gvt[:, 0:nch, :])
        tmpm = A.alloc(128, name="tmpm")
        for c in range(nch):
            for m in range(2):
                bk = 4 + (2 * c + m) % 2
                for hh in range(2):
                    hd = 2 * m + hh
                    P.MM(T(PS[:, bk, 128 * hh:128 * hh + 128], psb[bk]), vb[:, c, 128 * m:128 * m + 128], wsb[:, hd, :])
                for hh in range(2):
                    rows = slice(64 * hh, 64 * hh + 64)
                    P.TT(tmpm[rows, :], T(PS[rows, bk, 128 * hh:128 * hh + 128], psb[bk]), bsv[rows, l, m, :], ALU.add)
                P.TT(mix[:, 6 + m, 128 * c:128 * c + 128], tmpm, gu[:, m, 128 * c:128 * c + 128], ALU.mult)

        phase_done()
        dg = A.alloc(31 * 128, BF16, name="dg").re("p (k n) -> p k n", k=31)
        cy = A.alloc(2 * 512, name="cy").re("p (k n) -> p k n", k=2)
        if kind == "p":
            zv = zp[:, :, 0:542]
            for m in range(2):
                P.CP(zv[:, m, 0:30], ctail[l][:, m, :])
                P.CP(zv[:, m, 30:542], zsig[:, m, :])
                P.CP(ctail[l][:, m, :], zv[:, m, 512:542])
            if ti == NT - 1:
                for m in range(2):
                    P.DMA("pool", T(o_conv_p.ap[l, 128 * m:128 * m + 128, :], o_conv_p.buf), zsig[:, m, 482:512])
        else:
            zv4 = zp[:, :, 0:NSS * 94].re("p k (s n) -> p k s n", s=NSS)
            cbf = A.alloc(2 * NSS * 30, name="cbf").re("p (k s n) -> p k s n", k=2, s=NSS)
            P.DMA("pool", cbf, T(cbuf.ap[:, l], cbuf.buf))
            for m in range(2):
                P.CP(zv4[:, m, :, 0:30], cbf[:, m])
                P.CP(zv4[:, m, :, 30:94], zsig[:, m, 0:NS].re("p (s n) -> p s n", s=NSS))
                P.DMA("pool", T(o_conv_s.ap[l, :, 128 * m:128 * m + 128, :].rearrange("s p n -> p s n"), o_conv_s.buf),
                      zsig[:, m, 0:NS].re("p (s n) -> p s n", s=NSS)[:, :, 34:64])
        for m in range(2):
            for k in range(31):
                P.TS(dg[:, k, :], ident_b, T(convw_sb.ap[:, l, m, k:k + 1], convw_sb.buf))
            o = ps(m % 2, cols=N)
            for k in range(31):
                if kind == "p":
                    rhs = zv[:, m, k:k + 512]
                    P.MM(o, dg[:, k, :], rhs, start=(k == 0), stop=(k == 30))
                else:
                    rhs = zv4[:, m, :, k:k + 64]
                    P.MM(T(PS[:, m % 2, 0:NS].rearrange("p (s n) -> p s n", s=NSS), o.buf), dg[:, k, :], rhs, start=(k == 0), stop=(k == 30))
            P.ACT(cy[:, m, 0:N], o, AF.Identity, bias=cp[:, 2, m:m + 1])
        ln_feature_major(cy, N, cp[:, 3, :], cp[:, 4, :])
        cyb = A.alloc(2 * 512, BF16, name="cyb").re("p (k n) -> p k n", k=2)
        for m in range(2):
            P.ACT(cyb[:, m, 0:N], cy[:, m, 0:N], AF.Silu)
        wpw = A.alloc(2 * 256, BF16, name="wpw").re("p (k n) -> p k n", k=2)
        P.DMA("pool", wpw, kview(Wn["conv_w_pw"], l, 0, 256, 0, 256))
        for m in range(2):
            o = ps(2 + m % 2, cols=N)
            for k in range(2):
                P.MM(o, wpw[:, k, 128 * m:128 * m + 128], cyb[:, k, 0:N], start=(k == 0), stop=(k == 1))
            P.ACT(mix[:, 4 + m, 0:N], o, AF.Copy)

        phase_done()
        s5(l, kind, ti, N, nsub, cp)
        phase_done()

        attention(l, kind, ti, N, KTown if kind == "s" else None, VAown if kind == "s" else None)
        phase_done()

        A.reset()
        for c in range(2):
            w = w_next("wout")
            for mi in range(4):
                i = 4 * c + mi
                o = ps(i % 2, cols=N)
                for k in range(8):
                    P.MM(o, w[:, k, 128 * mi:128 * mi + 128], mix[:, k, 0:N], start=(k == 0), stop=(k == 7))
                P.TT(x[:, i, 0:N], x[:, i, 0:N], o, ALU.add)

    gm_sb = P.sb("gm_sb", [128, L, 2, 256])
    P.DMA("pool", gm_sb, gm_gb)

    def ln_feature_major(v, N, gT, bT):
        sq = A.alloc(2 * 512, name="lnsq").re("p (k n) -> p k n", k=2)
        for m in range(2):
            P.ACT(sq[:, m, 0:N], v[:, m, 0:N], AF.Square)
        mu = ps(4, cols=N)
        e2 = ps(5, cols=N)
        for m in range(2):
            P.MM(mu, onesf128, v[:, m, 0:N], start=(m == 0), stop=(m == 1))
        for m in range(2):
            P.MM(e2, onesf128, sq[:, m, 0:N], start=(m == 0), stop=(m == 1))
        var = A.alloc(512, name="lnvar")
        mus = A.alloc(512, name="lnmu")
        P.CP(mus[:, 0:N], mu)
        P.TT(var[:, 0:N], mus[:, 0:N], mus[:, 0:N], ALU.mult)
        P.TT(var[:, 0:N], e2, var[:, 0:N], ALU.subtract)
        P.ACT(var[:, 0:N], var[:, 0:N], AF.Sqrt, bias=EPS)
        P.RECIP(var[:, 0:N], var[:, 0:N])
        for m in range(2):
            P.TT(v[:, m, 0:N], v[:, m, 0:N], mus[:, 0:N], ALU.subtract)
            P.TT(v[:, m, 0:N], v[:, m, 0:N], var[:, 0:N], ALU.mult)
            P.TS(v[:, m, 0:N], v[:, m, 0:N], gT[:, m:m + 1], bT[:, m:m + 1], ALU.mult, ALU.add)

    def s5(l, kind, ti, N, nsub, cp):
        A.reset()
        TB = A.alloc(5 * 1024, name="TBl").re("p (a g j) -> p a g j", a=5, g=16)
        P.DMA("pool", TB.re("p a g j -> p (a g j)"), T(s5tab.ap[l], s5tab.buf))
        Bw = A.alloc(2 * 16 * 128, BF16, name="Bw").re("p (a g n) -> p a g n", a=2, g=16)
        Cw = A.alloc(2 * 16 * 128, BF16, name="Cw").re("p (a g n) -> p a g n", a=2, g=16)
        P.DMA("pool", Bw[:, 0], T(Bx_d.ap[l], Bx_d.buf))
        P.DMA("pool", Bw[:, 1], T(Bxs_d.ap[l], Bxs_d.buf))
        P.DMA("pool", Cw[:, 0], T(Cx1_d.ap[l], Cx1_d.buf))
        P.DMA("pool", Cw[:, 1], T(Cx2_d.ap[l], Cx2_d.buf))
        wgl = A.alloc(2 * 256, BF16, name="wgl").re("p (k n) -> p k n", k=2)
        P.DMA("pool", wgl, kview(Wn["ssm_w_glu"], l, 0, 256, 0, 256))
        s5mark = A.off
        bp = A.alloc(1024, name="bp").re("p (g j) -> p g j", g=16)
        t1 = A.alloc(1024, name="t1").re("p (g j) -> p g j", g=16)
        Sp = A.alloc(1024, name="Sp").re("p (g j) -> p g j", g=16)
        P12 = A.alloc(2 * 1024, BF16, name="P12").re("p (a g j) -> p a g j", a=2, g=16)
        init = A.alloc(16, name="init")
        sw = A.alloc(16, name="sw")
        t16 = A.alloc(16, name="t16")
        if kind == "s":
            s0sb = A.alloc(NSS * 16, name="s0sb").re("p (s g) -> p s g", s=NSS)
            P.DMA("pool", s0sb, T(s0.ap[:, l], s0.buf))
            send = A.alloc(NSS * 16, name="send").re("p (s g) -> p s g", s=NSS)
        ybuf = [Buf("yacc0", psum=True), Buf("yacc1", psum=True)]
        for sI in range(nsub):
            c0 = 64 * sI
            for hf in range(2):
                for g8 in range(8):
                    g = 8 * hf + g8
                    P.MM(T(PS[:, hf, 64 * g8:64 * g8 + 64], psb[hf]), Bw[:, 0, g, :], u_b[:, g // 8, c0:c0 + 64])
                for g8 in range(8):
                    g = 8 * hf + g8
                    P.MM(T(PS[:, 2 + hf, 64 * g8:64 * g8 + 64], psb[2 + hf]), Bw[:, 1, g, :], u_b[:, g // 8, c0:c0 + 64])
            for hf in range(2):
                gs = slice(8 * hf, 8 * hf + 8)
                Xv = T(PS[:, hf, :].rearrange("p (g j) -> p g j", g=8), psb[hf])
                Xsv = T(PS[:, 2 + hf, :].rearrange("p (g j) -> p g j", g=8), psb[2 + hf])
                P.TT(bp[:, gs, :], Xv, TB[:, 0, gs, :], ALU.mult)
                P.TT(t1[:, gs, :], Xsv, TB[:, 1, gs, :], ALU.mult)
            P.TT(bp, bp, t1, ALU.add)
            if kind == "p":
                ini = sst[l]
            else:
                ini = s0sb[:, sI, :]
            for g in range(16):
                P.SCAN(Sp[:, g, :], TB[:, 4, g, :], bp[:, g, :], ini[:, g:g + 1])
            last = Sp[:, :, 63]
            P.CP(sw[0:64, :], last[64:128, :])
            P.CP(sw[64:128, :], last[0:64, :])
            P.TT(t16, last, Etab[:, l, 0, :], ALU.mult)
            P.TT(sw, sw, Etab[:, l, 1, :], ALU.mult)
            if kind == "p":
                P.TT(sst[l], t16, sw, ALU.add)
            else:
                P.TT(send[:, sI, :], t16, sw, ALU.add)
            P.TT(P12[:, 0], Sp, TB[:, 2], ALU.mult)
            P.TT(P12[:, 1], Sp, TB[:, 3], ALU.mult)
            for m in range(2):
                yo = T(PS[:, 4 + m, c0:c0 + 64], ybuf[m])
                n = 0
                for g8 in range(8):
                    g = 8 * m + g8
                    for a in range(2):
                        P.MM(yo, Cw[:, a, g, :], P12[:, a, g, :], start=(n == 0), stop=(n == 15))
                        n += 1
        if kind == "p":
            if ti == NT - 1:
                P.DMA("pool", T(o_ssm_p.ap[l], o_ssm_p.buf), sst[l])
        else:
            P.DMA("pool", T(o_ssm_s.ap[l].rearrange("s p g -> p s g"), o_ssm_s.buf), send)
        S.barrier()
        A.off = s5mark
        gf = A.alloc(2 * 512, name="gf").re("p (k n) -> p k n", k=2)
        gb = A.alloc(2 * 512, BF16, name="gb").re("p (k n) -> p k n", k=2)
        for m in range(2):
            yo = T(PS[:, 4 + m, 0:N], ybuf[m])
            P.STT(gf[:, m, 0:N], u_f[:, m, 0:N], cp[:, 0, m:m + 1], yo, ALU.mult, ALU.add)
            P.ACT(gf[:, m, 0:N], gf[:, m, 0:N], AF.Gelu_apprx_tanh)
            P.CP(gb[:, m, 0:N], gf[:, m, 0:N])
        for m in range(2):
            o = ps(6 + m % 2, cols=N)
            for k in range(2):
                P.MM(o, wgl[:, k, 128 * m:128 * m + 128], gb[:, k, 0:N], start=(k == 0), stop=(k == 1))
            sgm = A.alloc(512, name="sgm%d" % m)
            P.ACT(sgm[:, 0:N], o, AF.Sigmoid, bias=cp[:, 1, m:m + 1])
            P.TT(mix[:, m, 0:N], gf[:, m, 0:N], sgm[:, 0:N], ALU.mult)

    def attention(l, kind, ti, N, KTown, VAown):
        A.reset()
        nq = 4 if kind == "p" else NSS
        QN = 128 if kind == "p" else 64
        accS = A.alloc(2 * 4 * N, parts=65, name="accS").re("p (a h q) -> p a h q", a=2, h=4)
        NSLOT = 4
        PT = A.alloc(NSLOT * 4 * 128, BF16, name="PT").re("p (s r q) -> p s r q", s=NSLOT, r=4)
        sbuf_ = [[Buf("sc_%d" % hh, psum=True) for hh in range(2)]] * NSLOT
        ptb = [[Buf("pt%d_%d" % (s_, hh)) for hh in range(2)] for s_ in range(NSLOT)]
        accb = [Buf("acc0", psum=True), Buf("acc1", psum=True)]
        if kind == "s":
            KTc = A.alloc(2 * PAST, BF16, name="KTc").re("p (k n) -> p k n", k=2)
            VAc = A.alloc(16 * 4 * 65, BF16, name="VAc").re("p (j h d) -> p j h d", j=16, h=4)
            P.MEMSET(VAc[:, :, :, 64:65], 1.0)
        step = 0
        for qi in range(nq):
            if kind == "p":
                Q = 4 * ti + qi
                keys = [("h", j, Q - j) for j in range(Q + 1)]
            else:
                P.DMA("pool", KTc, T(ckT.ap[l, qi].rearrange("(k p) n -> p k n", p=128), ckT.buf))
                for hd_ in range(4):
                    P.DMA("pool", VAc[:, :, hd_, 0:64], T(cv.ap[l, qi].rearrange("(j p) (h d) -> p j h d", p=128, h=4)[:, :, hd_, :], cv.buf))
                keys = [("c", j, 16 - j) for j in range(16)] + [("o", qi, 0)]
            for kt in range(2):
                ab = qi * 2 + kt
                accbank = 4 + ab % 2
                pend = None
                nsteps = len(keys)
                for si, (src, j, d) in enumerate(keys):
                    slot = step % NSLOT
                    step += 1
                    KP = 128 if src != "o" else 64
                    for r in range(4):
                        hh = r // 2
                        if src == "h":
                            lhs = KTh[32 * r:32 * r + 32, kt, 128 * j:128 * j + 128]
                        elif src == "c":
                            lhs = KTc[32 * r:32 * r + 32, kt, 128 * j:128 * j + 128]
                        else:
                            lhs = KTown[32 * r:32 * r + 32, kt, 64 * j:64 * j + 64]
                        rhs = QT[32 * r:32 * r + 32, kt, QN * qi:QN * qi + QN]
                        P.MM(T(PS[0:KP, r, 0:QN], sbuf_[slot][hh]), lhs, rhs, tile_position=(32 * r, 0))
                    if pend is not None:
                        pend()
                    for hh in range(2):
                        hd = 2 * kt + hh
                        sv = T(PS[0:KP, 2 * hh:2 * hh + 2, 0:QN], sbuf_[slot][hh])
                        pv = T(PT.ap[0:KP, slot, 2 * hh:2 * hh + 2, 0:QN], ptb[slot][hh])
                        if d == 0:
                            for a in range(2):
                                P.TT(T(PS[0:KP, 2 * hh + a, 0:QN], sbuf_[slot][hh]),
                                     T(PS[0:KP, 2 * hh + a, 0:QN], sbuf_[slot][hh]),
                                     diag[0:KP, hd, 0:QN], ALU.add)
                            P.ACT(pv, sv, AF.Exp)
                        else:
                            P.ACT(pv, sv, AF.Exp, bias=alibi[0:KP, hd, d:d + 1])

                    def av(slot=slot, src=src, j=j, KP=KP, first=(si == 0), kt=kt, accbank=accbank, ab=ab):
                        for r in range(4):
                            hh = r // 2
                            hd = 2 * kt + hh
                            if src == "h":
                                lhs = VAh[:, j, hd, :]
                            elif src == "c":
                                lhs = VAc[:, j, hd, :]
                            else:
                                lhs = VAown[:, j, hd, :]
                            P.MM(T(PS[0:65, accbank, 128 * r:128 * r + QN], accb[ab % 2]), lhs,
                                 T(PT.ap[0:KP, slot, r, 0:QN], ptb[slot][hh]),
                                 start=(first and r == 0), stop=True, skip_group_check=True)
                    pend = av
                pend()
                src_v = T(PS[0:65, accbank, :].rearrange("p (hh a q) -> p a hh q", hh=2, a=2)[:, :, :, 0:QN], accb[ab % 2])
                P.ACT(accS[:, :, 2 * kt:2 * kt + 2, QN * qi:QN * qi + QN], src_v, AF.Copy)
        Rz = A.alloc(512, parts=64, name="Rz")
        for a in range(2):
            for hd in range(4):
                zb = ps(6 + (2 * a + hd) % 2, parts=64, cols=N)
                P.MM(zb, sel, accS[:, a, hd, 0:N])
                P.RECIP(Rz[:, 0:N], zb)
                P.TT(accS[0:64, a, hd, 0:N], accS[0:64, a, hd, 0:N], Rz[:, 0:N], ALU.mult)
        Dd = A.alloc(4 * N, parts=64, name="Dd").re("p (h q) -> p h q", h=4)
        Dq = A.alloc(4 * N, parts=64, name="Dq").re("p (h q) -> p h q", h=4)
        P.STT(Dd[:, :, 0:N], accS[0:64, 1, :, 0:N], lamneg[:, l:l + 1], accS[0:64, 0, :, 0:N], ALU.mult, ALU.add)
        P.ACT(Dq[:, :, 0:N], Dd[:, :, 0:N], AF.Square)
        for hd in range(4):
            mo = ps(6 + hd % 2, parts=64, cols=N)
            P.MM(mo, onesf64, Dq[:, hd, 0:N])
            P.ACT(Rz[:, 0:N], mo, AF.Sqrt, bias=EPS)
            P.RECIP(Rz[:, 0:N], Rz[:, 0:N])
            P.TS(Dd[:, hd, 0:N], Dd[:, hd, 0:N], gcol[:, l:l + 1])
            hh = hd % 2
            P.TT(mix[64 * hh:64 * hh + 64, 2 + hd // 2, 0:N], Dd[:, hd, 0:N], Rz[:, 0:N], ALU.mult)

    def xattn(l, kind, N):
        A.reset()
        rmsnorm_to_h(N, gains_sb[:, l, 2, :])
        QX = A.alloc(4 * 512, BF16, name="QX").re("p (h n) -> p h n", h=4)
        ob = A.alloc(4 * 512, BF16, name="ob").re("p (h n) -> p h n", h=4)
        PX = [A.alloc(512, BF16, name="PX%d" % i) for i in range(2)]
        Rz = A.alloc(512, name="xRz")
        w = w_next("wq")
        for m in range(4):
            o = ps(m % 2, cols=N)
            for k in range(8):
                P.MM(o, w[:, k, 128 * m:128 * m + 128], h[:, k, 0:N], start=(k == 0), stop=(k == 7))
            P.TS(QX[:, m, 0:N], o, 128.0 ** -0.5)
        mk_s = A.alloc(4 * 256, BF16, name="mk_s").re("p (h n) -> p h n", h=4)
        mv_s = A.alloc(2 * 512, BF16, name="mv_s").re("p (c n) -> p c n", c=2)
        segs = [(0, N, None)] if kind == "p" else [(64 * s_, 64, s_) for s_ in range(NSS)]
        cnt = 0
        for (q0, qn, sidx) in segs:
            if sidx is None:
                P.DMA("pool", mk_s, T(o_mkT.ap[l].rearrange("(h p) n -> p h n", p=128), o_mkT.buf))
                P.DMA("pool", mv_s, T(o_mv.ap[l].rearrange("(c p) n -> p c n", p=128), o_mv.buf))
            else:
                P.DMA("pool", mk_s, T(cmkT.ap[l, sidx].rearrange("(h p) n -> p h n", p=128), cmkT.buf))
                P.DMA("pool", mv_s, T(cmv.ap[l, sidx].rearrange("(c p) n -> p c n", p=128), cmv.buf))
            mk, mv = mk_s, mv_s
            for hd in range(4):
                oacc = ps(4, cols=qn)
                zacc = ps(5, cols=qn)
                for c in range(2):
                    sc_ = ps(cnt % 2, cols=qn)
                    P.MM(sc_, mk[:, hd, 128 * c:128 * c + 128], QX[:, hd, q0:q0 + qn])
                    px = PX[cnt % 2]
                    cnt += 1
                    P.ACT(px[:, 0:qn], sc_, AF.Exp)
                    P.MM(oacc, mv[:, c, 128 * hd:128 * hd + 128], px[:, 0:qn], start=(c == 0), stop=(c == 1))
                    P.MM(zacc, ones_b, px[:, 0:qn], start=(c == 0), stop=(c == 1))
                P.RECIP(Rz[:, 0:qn], zacc)
                P.TT(ob[:, hd, q0:q0 + qn], oacc, Rz[:, 0:qn], ALU.mult)
        w = w_next("wo")
        for i in range(8):
            o = ps(2 + i % 2, cols=N)
            for k in range(4):
                P.MM(o, w[:, k, 128 * i:128 * i + 128], ob[:, k, 0:N], start=(k == 0), stop=(k == 3))
            P.TT(x[:, i, 0:N], x[:, i, 0:N], o, ALU.add)

    def main_program():
        for l_ in range(L):
            convert_weights(l_, only=["xattn_wk", "xattn_wv"])
        convert_weights(0)
        prep()
        phase_done()
        for (kind, ti) in tiles:
            N = 512 if kind == "p" else NS
            A.reset()
            if kind == "p":
                P.DMA("pool", x, T(xT_p.ap[:, 512 * ti:512 * ti + 512].rearrange("(k p) n -> p k n", p=128), xT_p.buf))
            else:
                P.DMA("pool", x[:, :, 0:N], T(xT_s.ap.rearrange("(k p) n -> p k n", p=128), xT_s.buf))
            for l in range(L):
                if (kind, ti) == tiles[0] and l + 1 < L:
                    convert_weights(l + 1)
                ffn(l, "ffn1", N, 0)
                phase_done()
                mixer(l, kind, ti)
                phase_done()
                xattn(l, kind, N)
                phase_done()
                ffn(l, "ffn2", N, 4)
                phase_done()
            A.reset()
            sq = A.alloc(8 * 512, BF16, name="fsq").re("p (k n) -> p k n", k=8)
            for k in range(8):
                P.ACT(sq[:, k, 0:N], x[:, k, 0:N], AF.Square)
            acc = ps(6, cols=N)
            for k in range(8):
                P.MM(acc, ones_b, sq[:, k, 0:N], start=(k == 0), stop=(k == 7))
            rs = A.alloc(512, name="frs")
            P.ACT(rs[:, 0:N], acc, AF.Sqrt, bias=EPS, scale=1.0 / D)
            P.RECIP(rs[:, 0:N], rs[:, 0:N])
            yo = A.alloc(8 * 512, name="yo").re("p (k n) -> p k n", k=8)
            for k in range(8):
                P.STT(yo[:, k, 0:N], x[:, k, 0:N], fin_sb[:, k:k + 1], rs[:, 0:N], ALU.mult, ALU.mult)
            if kind == "p":
                P.DMA("pool", T(yT_p.ap[:, 512 * ti:512 * ti + 512].rearrange("(k p) n -> p k n", p=128), yT_p.buf), yo)
            else:
                P.DMA("pool", T(yT_s.ap.rearrange("(k p) n -> p k n", p=128), yT_s.buf), yo[:, :, 0:N])
        assert wstate["taken"] == len(wq_list), (wstate, len(wq_list))

    try:
        main_program()
    except StopBuild:
        pass
    S.barrier()
    S.final_wait_all("sp")
    S.emit()
    P.st.close()
    return nc


def _consts(L):
    slopes = 2.0 ** (-8.0 * np.arange(1, 5, dtype=np.float64) / 4)
    kr = np.arange(128)[:, None, None]
    d = np.arange(33)[None, None, :]
    alibi = (slopes[None, :, None] * (kr - 128 * d - 64)).astype(np.float32)
    qr = np.arange(128)[None, None, :]
    vis = (kr // 64) <= (qr // 64)
    dg = -slopes[None, :, None] * np.abs(qr - kr) + slopes[None, :, None] * (qr - 64)
    diag = np.where(vis, dg, -30000.0).astype(np.float32)
    tril = (np.arange(128)[None, :] >= np.arange(128)[:, None]).astype(np.float32)
    sel = np.zeros((65, 64), np.float32)
    sel[64, :] = 1.0
    misc = np.zeros((128, 66), np.float32)
    misc[:, 0:64] = np.arange(1, 65, dtype=np.float32)[None, :]
    misc[0:64, 64] = -1.0
    misc[64:128, 64] = 1.0
    clam = np.zeros((64, L, 2), np.float32)
    for l in range(L):
        li = 0.8 - 0.6 * math.exp(-0.3 * l)
        clam[:, l, 0] = li
        clam[:, l, 1] = 1.0 - li
    return dict(c_ident=np.eye(128, dtype=np.float32), c_alibi=alibi, c_diag=diag, c_tril=tril, c_sel=sel,
                c_misc=misc, c_lam=clam)


def _fm(a):
    Ln, n = a.shape
    return np.ascontiguousarray(a.reshape(Ln, n // 128, 128).transpose(2, 0, 1))


def _shared_inputs(inp, L):
    f = lambda k: np.asarray(inp[k], dtype=np.float32)
    sh = {}
    for nm in ["ffn1_w_gate", "ffn1_w_up", "ffn1_w_down", "ffn2_w_gate", "ffn2_w_up", "ffn2_w_down", "w_in", "w_out",
               "xattn_wq", "xattn_wk", "xattn_wv", "xattn_wo", "ssm_w_glu", "conv_w_pw"]:
        sh[nm] = np.ascontiguousarray(f(nm)[:L])
    g = np.stack([_fm(f(n)[:L]) for n in ["ffn1_norm", "mix_norm", "xattn_norm", "mem_norm", "ffn2_norm"]], axis=2)
    sh["gains"] = np.ascontiguousarray(g)
    sh["fin_g"] = np.ascontiguousarray(f("final_norm").reshape(8, 128).T)
    bre, bim, cre, cim = f("ssm_b_re")[:L], f("ssm_b_im")[:L], f("ssm_c_re")[:L], f("ssm_c_im")[:L]
    Bx = np.zeros((L, 128, 16, 128), np.float32)
    Bxs = np.zeros((L, 128, 16, 128), np.float32)
    Cx1 = np.zeros((L, 128, 16, 128), np.float32)
    Cx2 = np.zeros((L, 128, 16, 128), np.float32)
    for g_ in range(16):
        r0 = 16 * (g_ % 8)
        Bx[:, r0:r0 + 16, g_, 0:64] = bre[:, g_].transpose(0, 2, 1)
        Bx[:, r0:r0 + 16, g_, 64:128] = bim[:, g_].transpose(0, 2, 1)
        Bxs[:, r0:r0 + 16, g_, 0:64] = bim[:, g_].transpose(0, 2, 1)
        Bxs[:, r0:r0 + 16, g_, 64:128] = bre[:, g_].transpose(0, 2, 1)
        Cx1[:, 0:64, g_, r0:r0 + 16] = cre[:, g_].transpose(0, 2, 1)
        Cx1[:, 64:128, g_, r0:r0 + 16] = cim[:, g_].transpose(0, 2, 1)
        Cx2[:, 0:64, g_, r0:r0 + 16] = cim[:, g_].transpose(0, 2, 1)
        Cx2[:, 64:128, g_, r0:r0 + 16] = cre[:, g_].transpose(0, 2, 1)
    sh.update(Bx=Bx, Bxs=Bxs, Cx1=Cx1, Cx2=Cx2)
    sc = np.zeros((128, L, 3, 16), np.float32)
    are, aim, ldt = f("ssm_a_re")[:L], f("ssm_a_im")[:L], f("ssm_log_dt")[:L]
    sc[0:64, :, 0, :] = are.transpose(2, 0, 1)
    sc[64:128, :, 0, :] = are.transpose(2, 0, 1)
    sc[0:64, :, 1, :] = aim.transpose(2, 0, 1)
    sc[64:128, :, 1, :] = aim.transpose(2, 0, 1)
    sc[:, :, 2, :] = ldt[None, :, :]
    sh["ssm_sc"] = sc
    colp = np.zeros((128, L, 7, 2), np.float32)
    colp[:, :, 0, :] = _fm(f("ssm_d")[:L].reshape(L, 256))
    colp[:, :, 1, :] = _fm(f("ssm_b_glu")[:L])
    colp[:, :, 2, :] = _fm(f("conv_b")[:L])
    colp[:, :, 3, :] = _fm(f("conv_ln_g")[:L])
    colp[:, :, 4, :] = _fm(f("conv_ln_b")[:L])
    sh["colp"] = colp
    cw = f("conv_w")[:L]
    sh["convw"] = np.ascontiguousarray(cw.reshape(L, 31, 2, 128).transpose(3, 0, 2, 1))
    gm = np.stack([f("gmlp_ln_g")[:L], f("gmlp_ln_b")[:L]], axis=1)
    sh["gm_gb"] = np.ascontiguousarray(np.broadcast_to(gm[None], (128, L, 2, 256)))
    sh["wsT"] = np.ascontiguousarray(f("gmlp_ws")[:L].transpose(0, 3, 1, 2))
    bs = f("gmlp_bs")[:L]
    bsb = np.zeros((128, L, 2, 128), np.float32)
    for m in range(2):
        for hh in range(2):
            bsb[64 * hh:64 * hh + 64, :, m, :] = bs[:, 2 * m + hh, :][None]
    sh["bs_bc"] = bsb
    bsb2 = np.zeros((128, L, 2, 128), np.float32)
    for m in range(2):
        for hh in range(2):
            bsb2[64 * hh:64 * hh + 64, :, m, 0:64] = bs[:, 2 * m + hh, 0:64][None]
            bsb2[64 * hh:64 * hh + 64, :, m, 64:128] = bs[:, 2 * m + hh, 0:64][None]
    sh["bs_bc_s"] = bsb2
    sh["lqk"] = np.ascontiguousarray(np.stack([f("dattn_lq1")[:L], f("dattn_lk1")[:L], f("dattn_lq2")[:L], f("dattn_lk2")[:L]], axis=1)[None])
    sh["dn"] = np.ascontiguousarray(f("dattn_norm")[:L].T)
    sh.update(_consts(L))
    return sh


_NC_CACHE = {}
_SIM_HOOK = None
_LIMIT = None
_DBG = ""


class StopBuild(Exception):
    pass


def run(inp, TP, L):
    f = lambda k: np.asarray(inp[k], dtype=np.float32)
    key = (TP, L)
    if key not in _NC_CACHE:
        _NC_CACHE[key] = build(TP, L)
    nc = _NC_CACHE[key]
    sh = _shared_inputs(inp, L)
    xp, xs, mem = f("x_prompt"), f("x_sample"), f("mem_prompt")
    ck, cvv = f("cache_attn_k")[:L], f("cache_attn_v")[:L]
    sre, sim, sconv = f("state_ssm_re")[:L], f("state_ssm_im")[:L], f("state_conv")[:L]
    cmk, cmvv = f("cache_mem_k")[:L], f("cache_mem_v")[:L]
    in_maps = []
    for c in range(8):
        b = c // 2
        ss = slice(NSS * c, NSS * c + NSS)
        m = dict(sh)
        m["xT_p"] = np.ascontiguousarray(xp[b].T)
        m["xT_s"] = np.ascontiguousarray(xs[ss].reshape(NSS * TS, D).T)
        m["memT"] = np.ascontiguousarray(mem[b].T)
        m["ckT"] = np.ascontiguousarray(ck[:, ss].reshape(L, NSS, PAST, 256).transpose(0, 1, 3, 2))
        m["cv"] = np.ascontiguousarray(cvv[:, ss].reshape(L, NSS, PAST, 256))
        s0 = np.zeros((128, L, NSS, 16), np.float32)
        s0[0:64] = sre[:, ss].transpose(3, 0, 1, 2)
        s0[64:128] = sim[:, ss].transpose(3, 0, 1, 2)
        m["s0"] = s0
        cb = sconv[:, ss]
        m["cbuf"] = np.ascontiguousarray(cb.reshape(L, NSS, 30, 2, 128).transpose(4, 0, 3, 1, 2))
        m["cmkT"] = np.ascontiguousarray(cmk[:, ss].reshape(L, NSS, 256, 512).transpose(0, 1, 3, 2))
        m["cmv"] = np.ascontiguousarray(cmvv[:, ss].reshape(L, NSS, 256, 512))
        in_maps.append(m)
    if _SIM_HOOK is not None:
        R = _SIM_HOOK(nc, in_maps)
    else:
        res = run_bass_kernel_spmd(nc, in_maps, core_ids=list(range(8)))
        R = res.results
    B = 4
    y_p = np.stack([R[2 * b]["yT_p"].T for b in range(B)])
    y_s = np.concatenate([R[c]["yT_s"].T.reshape(NSS, TS, D) for c in range(8)])
    p_k = np.stack([R[2 * b]["o_kT_p"].transpose(0, 2, 1).reshape(L, TP, 4, 64) for b in range(B)], axis=1)
    p_v = np.stack([R[2 * b]["o_v_p"].reshape(L, TP, 4, 64) for b in range(B)], axis=1)
    p_sre = np.stack([R[2 * b]["o_ssm_p"][:, 0:64, :].transpose(0, 2, 1) for b in range(B)], axis=1)
    p_sim = np.stack([R[2 * b]["o_ssm_p"][:, 64:128, :].transpose(0, 2, 1) for b in range(B)], axis=1)
    p_conv = np.stack([R[2 * b]["o_conv_p"].transpose(0, 2, 1) for b in range(B)], axis=1)
    p_mk = np.stack([R[2 * b]["o_mkT"].transpose(0, 2, 1).reshape(L, 256, 4, 128) for b in range(B)], axis=1)
    p_mv = np.stack([R[2 * b]["o_mv"].reshape(L, 256, 4, 128) for b in range(B)], axis=1)
    s_k = np.concatenate([R[c]["o_kT_s"].transpose(0, 2, 1).reshape(L, NSS, TS, 4, 64) for c in range(8)], axis=1)
    s_v = np.concatenate([R[c]["o_v_s"].reshape(L, NSS, TS, 4, 64) for c in range(8)], axis=1)
    s_sre = np.concatenate([R[c]["o_ssm_s"][:, :, 0:64, :].transpose(0, 1, 3, 2) for c in range(8)], axis=1)
    s_sim = np.concatenate([R[c]["o_ssm_s"][:, :, 64:128, :].transpose(0, 1, 3, 2) for c in range(8)], axis=1)
    s_conv = np.concatenate([R[c]["o_conv_s"].transpose(0, 1, 3, 2) for c in range(8)], axis=1)
    s_gv = np.concatenate([R[c]["o_gv_s"].reshape(L, NSS, TS, 256) for c in range(8)], axis=1)
    outs = (y_p, y_s, p_k, p_v, p_sre, p_sim, p_conv, p_mk, p_mv, s_k, s_v, s_sre, s_sim, s_conv, s_gv)
    return tuple(np.ascontiguousarray(o, dtype=np.float32) for o in outs)


def kernel(**inputs):
    return run(inputs, TP_FULL, L_FULL)
```
